# Optimizing a Trainium2 kernel written in Bass

```python
import jax, jax.numpy as jnp
from jax import lax
import numpy as np

D_MODEL = 4096
BATCH = 4
SEQ = 2048
DEPTH = 1
DEC_BATCH = 128
DEC_SEQ = 4
PAST_LEN = 16384
PAGE_SIZE = 128

N_MEM = 256
D_POOL = 3 * D_MODEL // 8
D_CONV = 3 * D_MODEL // 8
D_XATTN = D_MODEL - D_POOL - D_CONV
N_XHEADS = 4
XHEAD_DIM = D_XATTN // N_XHEADS
POOL_WINDOWS = (2, 4, 8, 16)
N_POOL_GROUPS = len(POOL_WINDOWS)
POOL_GROUP = D_POOL // N_POOL_GROUPS
POOL_STATE = max(POOL_WINDOWS) - 1
CONV_WIDTH = 31
CONV_STATE = CONV_WIDTH - 1
D_IN = 2 * D_POOL + 3 * D_CONV + 2 * D_XATTN
EPS = 1e-6

kernel_name = "pool_conv_memory_hybrid_step"


def rmsnorm(x, g):
    xf = x.astype(jnp.float32)
    y = xf * lax.rsqrt(jnp.mean(xf * xf, axis=-1, keepdims=True) + EPS) * g.astype(jnp.float32)
    return y.astype(x.dtype)


def layernorm(x, g, b):
    xf = x.astype(jnp.float32)
    mu = jnp.mean(xf, axis=-1, keepdims=True)
    var = jnp.mean(jnp.square(xf - mu), axis=-1, keepdims=True)
    y = (xf - mu) * lax.rsqrt(var + EPS) * g.astype(jnp.float32) + b.astype(jnp.float32)
    return y.astype(x.dtype)


def memory_kv(mem, mem_norm_g, w_mem_k, w_mem_v):
    h = rmsnorm(mem, mem_norm_g)
    b = mem.shape[0]
    k = (h @ w_mem_k).reshape(b, N_MEM, N_XHEADS, XHEAD_DIM)
    v = (h @ w_mem_v).reshape(b, N_MEM, N_XHEADS, XHEAD_DIM)
    return k, v


def pool_mix(u, state, start_pos, w_pool, pool_scale):
    b, t, _ = u.shape
    ext = jnp.concatenate([state.astype(u.dtype), u], axis=1).astype(jnp.float32)
    cs = jnp.pad(jnp.cumsum(ext, axis=1), ((0, 0), (1, 0), (0, 0)))
    end = cs[:, POOL_STATE + 1:POOL_STATE + 1 + t]
    pos = start_pos + jnp.arange(t, dtype=jnp.int32)
    means = []
    for gi, w in enumerate(POOL_WINDOWS):
        lo_c, hi_c = gi * POOL_GROUP, (gi + 1) * POOL_GROUP
        lo = cs[:, POOL_STATE + 1 - w:POOL_STATE + 1 - w + t, lo_c:hi_c]
        cnt = jnp.minimum(w, pos + 1).astype(jnp.float32)[None, :, None]
        means.append((end[..., lo_c:hi_c] - lo) / cnt)
    pooled = jnp.concatenate(means, axis=-1) - ext[:, POOL_STATE:]
    mixed = jnp.einsum('btgc,gcd->btgd', pooled.reshape(b, t, N_POOL_GROUPS, POOL_GROUP),
                       w_pool.astype(jnp.float32)).reshape(b, t, D_POOL)
    mixed = mixed * pool_scale.astype(jnp.float32)
    return mixed.astype(u.dtype), ext[:, -POOL_STATE:].astype(u.dtype)


def conv_module(a, state, w_dw, b_dw, ln_g, ln_b, w_pw):
    ext = jnp.concatenate([state.astype(a.dtype), a], axis=1)
    y = lax.conv_general_dilated(ext, w_dw.astype(a.dtype)[:, None, :], window_strides=(1,), padding='VALID',
                                 dimension_numbers=('NWC', 'WIO', 'NWC'), feature_group_count=D_CONV)
    y = y + b_dw
    y = layernorm(y, ln_g, ln_b)
    y = jax.nn.silu(y) @ w_pw
    return y, ext[:, -CONV_STATE:]


def memory_attention(q, k, v):
    s = jnp.einsum('bthe,bmhe->bhtm', q.astype(jnp.float32), k.astype(jnp.float32)) * (XHEAD_DIM ** -0.5)
    p = jax.nn.softmax(s, axis=-1)
    o = jnp.einsum('bhtm,bmhe->bthe', p, v.astype(jnp.float32))
    b, t = q.shape[0], q.shape[1]
    return o.reshape(b, t, D_XATTN).astype(q.dtype)


def hybrid_layer(x, mem_k, mem_v, pool_state, conv_state, start_pos, norm_g, w_in, w_pool, pool_scale,
                 w_dw, b_dw, conv_ln_g, conv_ln_b, w_pw, w_out):
    b, t, _ = x.shape
    h = rmsnorm(x, norm_g)
    z = h @ w_in
    splits = np.cumsum([D_POOL, D_POOL, D_CONV, D_CONV, D_CONV, D_XATTN]).tolist()
    u_a, gate_a, glu_val, glu_gate, gate_b, q, gate_c = jnp.split(z, splits, axis=-1)
    o_a, new_pool = pool_mix(u_a, pool_state, start_pos, w_pool, pool_scale)
    o_a = o_a * jax.nn.silu(gate_a)
    a = glu_val * jax.nn.sigmoid(glu_gate)
    o_b, new_conv = conv_module(a, conv_state, w_dw, b_dw, conv_ln_g, conv_ln_b, w_pw)
    o_b = o_b * jax.nn.silu(gate_b)
    o_c = memory_attention(q.reshape(b, t, N_XHEADS, XHEAD_DIM), mem_k, mem_v) * jax.nn.silu(gate_c)
    y = x + jnp.concatenate([o_a, o_b, o_c], axis=-1) @ w_out
    return y, new_pool, new_conv


def setup_inputs(seed: int = 0) -> dict:
    key = jax.random.key(seed)
    ks = jax.random.split(key, 24)
    f32 = jnp.float32
    nrm = lambda k, shape, scale: jax.random.normal(k, shape, f32) * scale
    return {
        "x_prompt": nrm(ks[0], (BATCH, SEQ, D_MODEL), 1.0),
        "mem_prompt": nrm(ks[1], (BATCH, N_MEM, D_MODEL), 1.0),
        "x_sample": nrm(ks[2], (DEC_BATCH, DEC_SEQ, D_MODEL), 1.0),
        "cache_mem_k": nrm(ks[3], (DEPTH, DEC_BATCH, N_MEM, N_XHEADS, XHEAD_DIM), 1.0),
        "cache_mem_v": nrm(ks[4], (DEPTH, DEC_BATCH, N_MEM, N_XHEADS, XHEAD_DIM), 1.0),
        "state_pool": nrm(ks[5], (DEPTH, DEC_BATCH, POOL_STATE, D_POOL), 1.0),
        "state_conv": nrm(ks[6], (DEPTH, DEC_BATCH, CONV_STATE, D_CONV), 0.5),
        "norm_g": 1.0 + nrm(ks[7], (DEPTH, D_MODEL), 0.02),
        "mem_norm_g": 1.0 + nrm(ks[8], (DEPTH, D_MODEL), 0.02),
        "w_in": nrm(ks[9], (DEPTH, D_MODEL, D_IN), D_MODEL ** -0.5),
        "w_mem_k": nrm(ks[10], (DEPTH, D_MODEL, D_XATTN), D_MODEL ** -0.5),
        "w_mem_v": nrm(ks[11], (DEPTH, D_MODEL, D_XATTN), D_MODEL ** -0.5),
        "w_pool": nrm(ks[12], (DEPTH, N_POOL_GROUPS, POOL_GROUP, POOL_GROUP), POOL_GROUP ** -0.5),
        "pool_scale": 1.0 + nrm(ks[13], (DEPTH, D_POOL), 0.02),
        "w_dw": nrm(ks[14], (DEPTH, CONV_WIDTH, D_CONV), CONV_WIDTH ** -0.5),
        "b_dw": nrm(ks[15], (DEPTH, D_CONV), 0.01),
        "conv_ln_g": 1.0 + nrm(ks[16], (DEPTH, D_CONV), 0.02),
        "conv_ln_b": nrm(ks[17], (DEPTH, D_CONV), 0.01),
        "w_pw": nrm(ks[18], (DEPTH, D_CONV, D_CONV), D_CONV ** -0.5),
        "w_out": nrm(ks[19], (DEPTH, D_MODEL, D_MODEL), D_MODEL ** -0.5),
        "final_norm_g": 1.0 + nrm(ks[20], (D_MODEL,), 0.02),
    }


def reference(x_prompt, mem_prompt, x_sample, cache_mem_k, cache_mem_v, state_pool, state_conv,
              norm_g, mem_norm_g, w_in, w_mem_k, w_mem_v, w_pool, pool_scale, w_dw, b_dw,
              conv_ln_g, conv_ln_b, w_pw, w_out, final_norm_g):
    hp, hs = x_prompt, x_sample
    mk_p, mv_p, pool_p, conv_p, pool_s, conv_s = [], [], [], [], [], []
    for l in range(DEPTH):
        k_p, v_p = memory_kv(mem_prompt, mem_norm_g[l], w_mem_k[l], w_mem_v[l])
        zero_pool = jnp.zeros((BATCH, POOL_STATE, D_POOL), hp.dtype)
        zero_conv = jnp.zeros((BATCH, CONV_STATE, D_CONV), hp.dtype)
        hp, np_p, nc_p = hybrid_layer(hp, k_p, v_p, zero_pool, zero_conv, 0, norm_g[l], w_in[l], w_pool[l],
                                      pool_scale[l], w_dw[l], b_dw[l], conv_ln_g[l], conv_ln_b[l], w_pw[l], w_out[l])
        hs, np_s, nc_s = hybrid_layer(hs, cache_mem_k[l], cache_mem_v[l], state_pool[l], state_conv[l], PAST_LEN,
                                      norm_g[l], w_in[l], w_pool[l], pool_scale[l], w_dw[l], b_dw[l],
                                      conv_ln_g[l], conv_ln_b[l], w_pw[l], w_out[l])
        mk_p.append(k_p)
        mv_p.append(v_p)
        pool_p.append(np_p)
        conv_p.append(nc_p)
        pool_s.append(np_s)
        conv_s.append(nc_s)
    y_prompt = rmsnorm(hp, final_norm_g)
    y_sample = rmsnorm(hs, final_norm_g)
    return (y_prompt, y_sample, jnp.stack(mk_p), jnp.stack(mv_p), jnp.stack(pool_p), jnp.stack(conv_p),
            jnp.stack(pool_s), jnp.stack(conv_s))
```

```python
import contextlib
import numpy as np
import concourse.bass as bass
import concourse.mybir as mybir
from concourse.bass_utils import run_bass_kernel_spmd

F32 = mybir.dt.float32
BF16 = mybir.dt.bfloat16
AF = mybir.ActivationFunctionType
ALU = mybir.AluOpType
AX = mybir.AxisListType

ENGS = ("pe", "act", "dve", "pool", "sp")
NCORES = 8
D = 4096
DIN = 9728
NT = 576
W = 544
NPASS = 2
EPS = 1e-6
NSLOT = 4
POOLW = (2, 4, 8, 16)
C_G, C_MG, C_PSC, C_BDW, C_LNG, C_LNB, C_WDW, C_INV, C_END = 0, 32, 64, 76, 88, 100, 112, 484, 612
COL_U, COL_GA, COL_V, COL_GG, COL_GB, COL_Q, COL_GC = 0, 1536, 3072, 4608, 6144, 7680, 8704


class Buf:
    __slots__ = ("name", "w", "r", "dsem", "psum", "nobar")

    def __init__(self, name, psum=False, nobar=False):
        self.nobar = nobar
        self.name = name
        self.w = []
        self.r = []
        self.dsem = None
        self.psum = psum


class Sched:
    def __init__(self, nc, stack):
        self.nc = nc
        self.stack = stack
        self.h = {"pe": nc.tensor, "act": nc.scalar, "dve": nc.vector, "pool": nc.gpsimd, "sp": nc.sync}
        self.sems = {}
        self.cnt = {}
        self.known = {e: {} for e in ENGS}
        self.nobar = set()
        self.epoch = 0
        self.nsem = 0

    def _sem(self, key):
        if key not in self.sems:
            self.sems[key] = self.stack.enter_context(self.nc.semaphore("s%d" % self.nsem))
            self.nsem += 1
            self.cnt[key] = 0
        return self.sems[key]

    def new_epoch(self):
        self.epoch += 1

    def _wait(self, eng, ev):
        key, val, src = ev
        if src == eng and eng == "pe":
            return
        if self.known[eng].get(key, 0) >= val:
            return
        self.known[eng][key] = val
        self.h[eng].wait_ge(self.sems[key], val)

    def deps(self, eng, r, w, isdma=False):
        need = {}

        def add(ev):
            key, val, src = ev
            if src == eng and eng == "pe":
                return
            if val > need.get(key, 0):
                need[key] = val
        for b in r:
            for ev in b.w:
                add(ev)
            if b.psum:
                for ev in b.r:
                    if ev[2] != eng:
                        add(ev)
        for b in w:
            for ev in b.w:
                add(ev)
            for ev in b.r:
                add(ev)
        for key, val in need.items():
            self._wait(eng, (key, val, None))

    def op(self, eng, fn, r=(), w=()):
        self.deps(eng, r, w)
        key = ("eng", eng, self.epoch)
        sem = self._sem(key)
        self.cnt[key] += 1
        fn(self.h[eng]).then_inc(sem, 1)
        ev = (key, self.cnt[key], eng)
        for b in r:
            b.r.append(ev)
        for b in w:
            b.w = [ev]
            b.r = []
        return ev

    def dma(self, eng, fn, r=(), w=(), wa=(), sembuf=None):
        self.deps(eng, r, tuple(w) + tuple(wa), isdma=True)
        sb = sembuf or (w[0] if w else r[0])
        if sb.dsem is None:
            sb.dsem = ("dma", sb.name, id(sb))
            if sb.nobar:
                self.nobar.add(sb.dsem)
        sem = self._sem(sb.dsem)
        self.cnt[sb.dsem] += 16
        fn(self.h[eng]).then_inc(sem, 16)
        ev = (sb.dsem, self.cnt[sb.dsem], "dma")
        for b in r:
            b.r.append(ev)
        for b in w:
            b.w = [ev]
            b.r = []
        for b in wa:
            b.w.append(ev)
        return ev

    def barrier(self, pe_waits=False):
        for eng in ENGS:
            if eng == "pe" and not pe_waits:
                continue
            for key in list(self.cnt):
                if key in self.nobar and not pe_waits:
                    continue
                if self.cnt[key] > 0:
                    self._wait(eng, (key, self.cnt[key], None))


def build_program(phases=("kv", "a", "b", "c", "out"), loads_rec=None):
    nc = bass.Bass("TRN2", target_bir_lowering=False)

    def din(name, shape):
        return nc.dram_tensor(name, list(shape), F32, kind="ExternalInput").ap()

    def dout(name, shape):
        return nc.dram_tensor(name, list(shape), F32, kind="ExternalOutput").ap()

    x_d = din("x", [NPASS, NT, D])
    mem_d = din("mem", [256, D])
    ck_d = din("ck", [16, 256, 1024])
    cv_d = din("cv", [16, 256, 1024])
    spin_d = din("sp_in", [16, 15, 1536])
    scin_d = din("sc_in", [16, 30, 1536])
    cs_d = din("consts", [128, C_END])
    id_d = din("ident", [128, 128])
    fg_d = din("fg", [128, D])
    w_in_d = din("w_in", [D, DIN])
    w_mk_d = din("w_mem_k", [D, 1024])
    w_mv_d = din("w_mem_v", [D, 1024])
    w_pool_d = din("w_pool", [4, 384, 384])
    w_pw_d = din("w_pw", [1536, 1536])
    w_out_d = din("w_out", [D, D])

    y_d = dout("y", [NPASS, W, D])
    mk_d = dout("mk", [256, 1024])
    mv_d = dout("mv", [256, 1024])
    psp_d = dout("ps_p", [NPASS, 15, 1536])
    csp_d = dout("cs_p", [NPASS, 30, 1536])
    pss_d = dout("ps_s", [16, 15, 1536])
    css_d = dout("cs_s", [16, 30, 1536])

    w_in_v = w_in_d.rearrange("(kc p) c -> p kc c", p=128)
    w_mk_v = w_mk_d.rearrange("(kc p) c -> p kc c", p=128)
    w_mv_v = w_mv_d.rearrange("(kc p) c -> p kc c", p=128)
    w_pw_v = w_pw_d.rearrange("(kc p) c -> p kc c", p=128)
    w_out_v = w_out_d.rearrange("(kc p) c -> p kc c", p=128)

    with contextlib.ExitStack() as st:
        S = Sched(nc, st)

        tcount = [0]

        def T(name, shape, dt, stack=st):
            tcount[0] += 1
            return stack.enter_context(nc.sbuf_tensor("%s_%d" % (name, tcount[0]), list(shape), dt))

        HT = T("HT", [128, 32, NT], BF16)
        CC = T("CC", [128, 32, W], BF16)
        slots = [T("slot%d" % i, [128, 8192], BF16) for i in range(NSLOT)]
        CS = T("CS", [128, C_END], F32)
        IDF = T("IDF", [128, 128], F32)
        IDB = T("IDB", [128, 128], BF16)
        ONES = T("ONES", [128, 128], BF16)
        HI = [T("HI%d" % i, [128, 544], BF16) for i in range(2)]
        LO = [T("LO%d" % i, [128, 544], BF16) for i in range(2)]
        HLb = [Buf("HL0"), Buf("HL1")]
        hl_n = [0]
        KT = T("KT", [128, 8, 256], BF16)
        VB = T("VB", [128, 2, 1024], BF16)
        HTb, CSb, IDb, KTb, VBb = Buf("HT"), Buf("CS"), Buf("ID"), Buf("KT"), Buf("VB")
        CCb = [Buf("CC%d" % i) for i in range(32)]
        slotb = [Buf("slot%d" % i, nobar=True) for i in range(NSLOT)]

        PZ = [st.enter_context(nc.psum_tensor("pz%d" % i, [128, 1024], F32)) for i in range(3)]
        PZb = [Buf("pz%d" % i, True) for i in range(3)]
        PQb = [[Buf("pz%dq%d" % (i, q), True) for q in range(4)] for i in range(3)]
        PBF = [st.enter_context(nc.psum_tensor("pb%d" % i, [128, 512], F32)) for i in range(2)]
        PB = [t_.bitcast(BF16) for t_ in PBF]
        PBb = [Buf("pb%d" % i, True) for i in range(2)]
        rot = {"pz": 0, "pb": 0, "pq": 0}

        def pz():
            i = rot["pz"] % 3
            rot["pz"] += 1
            return PZ[i], PZb[i]

        def pb():
            i = rot["pb"] % 2
            rot["pb"] += 1
            return PB[i], PBb[i]

        UMODE = ["all"]

        def pq():
            if UMODE[0] == "pz":
                cands = [0, 1, 2]
            elif UMODE[0] == "pb":
                cands = [3, 4]
            else:
                cands = [0, 1, 2, 3, 4]
            i = cands[rot["pq"] % len(cands)]
            rot["pq"] += 1
            if i < 3:
                return PZ[i][:, 0:256], PZb[i]
            return PBF[i - 3][:, 0:256], PBb[i - 3]

        WV = {"w_in": w_in_v, "w_mk": w_mk_v, "w_mv": w_mv_v, "w_pw": w_pw_v, "w_out": w_out_v}
        w_in_r = w_in_d[:, 0:9216].rearrange("(kc p) (r c) -> p kc r c", p=128, c=1536)
        recording = loads_rec is None
        rec = []
        loads = loads_rec or []
        wst = {"issue": 0, "get": 0}

        free_slots = list(range(NSLOT))
        load_slot = {}

        def w_fill(limit=None):
            if recording:
                return
            nfill = 0
            while free_slots and wst["issue"] < len(loads):
                if limit is not None and nfill >= limit:
                    return
                nfill += 1
                k = wst["issue"]
                wst["issue"] += 1
                name, c0, n, K = loads[k]
                s = free_slots.pop(0)
                load_slot[k] = s
                if name == "w_in_vg":
                    dv = slots[s][:, 0:K * n].rearrange("p (k r c) -> p k r c", k=K, r=2)
                    for r_, colb in enumerate((COL_V, COL_GG)):
                        src = w_in_v[:, :, colb + c0 * 128:colb + (c0 + 1) * 128]
                        dst = dv[:, :, r_, :]
                        if r_ == 0:
                            S.dma("pool", lambda e, dst=dst, src=src: e.dma_start(out=dst, in_=src), w=[slotb[s]])
                        else:
                            S.dma("pool", lambda e, dst=dst, src=src: e.dma_start(out=dst, in_=src), wa=[slotb[s]], sembuf=slotb[s])
                    continue
                src = WV[name][:, :, c0:c0 + n]
                dst = slots[s][:, 0:K * n].rearrange("p (k n) -> p k n", k=K)
                S.dma("pool", lambda e: e.dma_start(out=dst, in_=src), w=[slotb[s]])

        def w_get(name, c0, n=256, K=32):
            k = wst["get"]
            wst["get"] += 1
            if recording:
                rec.append((name, c0, n, K))
                s = k % NSLOT
            else:
                assert loads[k] == (name, c0, n, K), (k, loads[k], (name, c0, n, K))
                assert k in load_slot, "weight slot starvation at load %d" % k
                s = load_slot[k]
            return slots[s][:, 0:K * n].rearrange("p (k n) -> p k n", k=K), slotb[s], s

        def w_release(s):
            if recording:
                return
            free_slots.append(s)
            w_fill()

        S.dma("sp", lambda e: e.dma_start(out=CS[:], in_=cs_d), w=[CSb])
        S.dma("sp", lambda e: e.dma_start(out=IDF[:], in_=id_d), w=[IDb])
        S.op("dve", lambda e: e.tensor_copy(out=IDB[:], in_=IDF[:]), r=[IDb], w=[IDb])
        S.op("dve", lambda e: e.memset(ONES[:], 1.0), w=[IDb])
        w_fill(limit=1)

        def col(c):
            return CS[:, c:c + 1]

        def split(src, srcb, rows, n):
            k = hl_n[0] % 2
            hl_n[0] += 1
            S.op("act", lambda e: e.copy(out=HI[k][0:rows, 0:n], in_=src), r=[srcb], w=[HLb[k]])
            S.op("dve", lambda e: e.tensor_tensor(out=LO[k][0:rows, 0:n], in0=src, in1=HI[k][0:rows, 0:n], op=ALU.subtract),
                 r=[srcb, HLb[k]], w=[HLb[k]])
            return HI[k], LO[k], HLb[k]

        def tr32(out, outb, hi, lo, hlb, rows, c0, c1):
            S.op("pe", lambda e: e.matmul(out, lhsT=hi[0:rows, c0:c1], rhs=IDB[0:rows, 0:rows], start=True, stop=False),
                 r=[hlb, IDb], w=[outb])
            S.op("pe", lambda e: e.matmul(out, lhsT=lo[0:rows, c0:c1], rhs=IDB[0:rows, 0:rows], start=False, stop=True),
                 r=[hlb, IDb], w=[outb])

        def proj(view, sb, ci, K, c0, rhs_t, rhs_b, ncols):
            t, tb = pz()
            segs = []
            a = c0
            while a < ncols:
                b = min(a + 512, ncols)
                segs.append((a, b))
                a = b
            for kc in range(K):
                for (a, b) in segs:
                    S.op("pe", lambda e, kc=kc, a=a, b=b: e.matmul(
                        t[:, a - c0:b - c0], lhsT=view[:, kc, ci * 128:(ci + 1) * 128], rhs=rhs_t[:, kc, a:b],
                        start=(kc == 0), stop=(kc == K - 1)), r=[sb, rhs_b], w=[tb])
            return t, tb

        def norm_transpose(src_fn, tiles, gcol, dst, dstb, A, tagp, nbuf=2, npull=0):
            xt = [T("xt%d" % i, [128, D], F32, A) for i in range(nbuf)] * (2 // nbuf)
            xn = [T("xn%d" % i, [128, D], BF16, A) for i in range(nbuf)] * (2 // nbuf)
            st_ = T("nst", [128, 24], F32, A)
            xtb = [Buf("xt%d" % i) for i in range(nbuf)] * (2 // nbuf)
            xnb = [[Buf("xn%d_%d" % (i, hh)) for hh in range(2)] for i in range(nbuf)] * (2 // nbuf)
            stb = Buf("nst")
            for i, (r0, rows) in enumerate(tiles):
                b = i % 2
                S.dma("pool" if npull else "sp", lambda e, b=b, r0=r0, rows=rows: e.dma_start(out=xt[b][0:rows, :], in_=src_fn(r0, rows)),
                      w=[xtb[b]])
                S.op("act", lambda e, b=b, rows=rows, i=i: e.activation(
                    out=xn[b][0:rows, :], in_=xt[b][0:rows, :], func=AF.Square, accum_out=st_[0:rows, i:i + 1]),
                    r=[xtb[b]], w=[xnb[b][0], xnb[b][1], stb])
                S.op("act", lambda e, rows=rows, i=i: e.activation(
                    out=st_[0:rows, 8 + i:9 + i], in_=st_[0:rows, i:i + 1], func=AF.Sqrt, scale=1.0 / D, bias=EPS),
                    r=[stb], w=[stb])
                S.op("dve", lambda e, rows=rows, i=i: e.reciprocal(out=st_[0:rows, 16 + i:17 + i], in_=st_[0:rows, 8 + i:9 + i]),
                     r=[stb], w=[stb])
                S.op("act", lambda e, b=b, rows=rows, i=i: e.activation(
                    out=xn[b][0:rows, 0:2048], in_=xt[b][0:rows, 0:2048], func=AF.Copy, scale=st_[0:rows, 16 + i:17 + i]),
                    r=[xtb[b], stb], w=[xnb[b][0]])
                S.op("dve", lambda e, b=b, rows=rows, i=i: e.tensor_scalar(
                    out=xn[b][0:rows, 2048:4096], in0=xt[b][0:rows, 2048:4096], scalar1=st_[0:rows, 16 + i:17 + i], scalar2=None,
                    op0=ALU.mult), r=[xtb[b], stb], w=[xnb[b][1]])
                for q in range(4):
                    t, tb = pb()
                    tv = t[:, :].rearrange("p (k n) -> p k n", k=8)
                    for k in range(8):
                        kc = q * 8 + k
                        S.op("pe", lambda e, k=k, kc=kc, b=b, rows=rows: e.transpose(
                            out=tv[:, k, 0:rows], in_=xn[b][0:rows, kc * 128:(kc + 1) * 128],
                            identity=IDB[0:rows, 0:rows]), r=[xnb[b][q // 2], IDb], w=[tb])
                    S.op("dve", lambda e, q=q, r0=r0, rows=rows: e.tensor_tensor(
                        out=dst[:, q * 8:(q + 1) * 8, r0:r0 + rows], in0=tv[:, :, 0:rows],
                        in1=CS[:, gcol + q * 8:gcol + q * 8 + 8][:, :, None].broadcast_to([128, 8, rows]),
                        op=ALU.mult), r=[tb, CSb], w=[dstb])
                    if npull:
                        do_pull(1)

        XT_TILES = [(0, 128), (128, 128), (256, 128), (384, 128), (512, 64)]
        OT_TILES = [(0, 128), (128, 128), (256, 128), (384, 128), (512, 32)]


        XR = [T("XR%d" % i, [128, 256], F32) for i in range(2)]
        YT = [T("YT%d" % i, [128, 256], F32) for i in range(2)]
        JK = T("JK", [128, 256], BF16)
        SSQ = [T("SSQ%d" % i, [128, 5, 16], F32) for i in range(NPASS)]
        R2 = T("R2", [128, 16], F32)
        XRb = [Buf("XR%d" % i, nobar=True) for i in range(2)]
        YTb = [Buf("YT%d" % i, nobar=True) for i in range(2)]
        JKb, R2b = Buf("JK"), Buf("R2")
        SSQb = [Buf("SSQ%d" % i) for i in range(NPASS)]
        ydbs = [[Buf("yd%d_%d" % (p_, i)) for i in range(5)] for p_ in range(NPASS)]

        def out_main(p, YL=None, YLb=None):
            ydb = ydbs[p]
            units = [(ds, ti) for ds in range(16) for ti in range(5)]

            def xr_load(n):
                if n >= len(units):
                    return
                ds, ti = units[n]
                t0, rows = OT_TILES[ti]
                S.dma("sp", lambda e: e.dma_start(
                    out=XR[n % 2][0:rows, :], in_=x_d[p, 32 + t0:32 + t0 + rows, ds * 256:(ds + 1) * 256]), w=[XRb[n % 2]])
            xr_load(0)
            view = sb = None
            for n, (ds, ti) in enumerate(units):
                if ti == 0:
                    view, sb, sl = w_get("w_out", ds * 256)
                t0, rows = OT_TILES[ti]
                xr_load(n + 1)
                o, ob = pq()
                for kc in range(32):
                    S.op("pe", lambda e, kc=kc, o=o, view=view: e.matmul(
                        o[0:rows, :], lhsT=CC[:, kc, t0:t0 + rows], rhs=view[:, kc, :], start=(kc == 0), stop=(kc == 31)),
                        r=[sb, CCb[kc]], w=[ob])
                yb = n % 2
                if YL is not None:
                    ydst = YL[ti][0:rows, ds * 256:(ds + 1) * 256]
                    S.op("dve", lambda e, o=o, n=n, ydst=ydst: e.tensor_tensor(out=ydst, in0=o[0:rows, :],
                                                                               in1=XR[n % 2][0:rows, :], op=ALU.add),
                         r=[ob, XRb[n % 2], YLb[ti]], w=[YLb[ti]])
                    S.op("act", lambda e, ti=ti, ds=ds, ydst=ydst: e.activation(
                        out=JK[0:rows, :], in_=ydst, func=AF.Square, accum_out=SSQ[p][0:rows, ti, ds:ds + 1]),
                        r=[YLb[ti]], w=[JKb, SSQb[p]])
                    if ti == 4:
                        w_release(sl)
                    yield
                    continue
                S.op("dve", lambda e, o=o, yb=yb, n=n: e.tensor_tensor(out=YT[yb][0:rows, :], in0=o[0:rows, :],
                                                                       in1=XR[n % 2][0:rows, :], op=ALU.add),
                     r=[ob, XRb[n % 2]], w=[YTb[yb]])
                S.op("act", lambda e, yb=yb, ti=ti, ds=ds: e.activation(
                    out=JK[0:rows, :], in_=YT[yb][0:rows, :], func=AF.Square, accum_out=SSQ[p][0:rows, ti, ds:ds + 1]),
                    r=[YTb[yb]], w=[JKb, SSQb[p]])
                S.dma("act", lambda e, yb=yb, ds=ds: e.dma_start(
                    out=y_d[p, t0:t0 + rows, ds * 256:(ds + 1) * 256], in_=YT[yb][0:rows, :]),
                    r=[YTb[yb]], wa=[ydb[ti]], sembuf=YTb[yb])
                if ti == 4:
                    w_release(sl)
                yield

        def out_final(p, A, CBW=512):
            ydb = ydbs[p]
            YF = [T("YF%d" % i, [128, CBW], F32, A) for i in range(2)]
            FG = [T("FG%d" % i, [128, CBW], F32, A) for i in range(2)]
            YFb = [Buf("YF0"), Buf("YF1")]
            FGb = [Buf("FG0"), Buf("FG1")]
            for ti, (t0, rows) in enumerate(OT_TILES):
                S.op("dve", lambda e, ti=ti: e.tensor_reduce(out=R2[0:rows, ti:ti + 1], in_=SSQ[p][0:rows, ti, :], axis=AX.X, op=ALU.add),
                     r=[SSQb[p]], w=[R2b])
                S.op("act", lambda e, ti=ti: e.activation(out=R2[0:rows, 5 + ti:6 + ti], in_=R2[0:rows, ti:ti + 1], func=AF.Sqrt,
                                                          scale=1.0 / D, bias=EPS), r=[R2b], w=[R2b])
                S.op("dve", lambda e, ti=ti: e.reciprocal(out=R2[0:rows, 10 + ti:11 + ti], in_=R2[0:rows, 5 + ti:6 + ti]),
                     r=[R2b], w=[R2b])
            n = 0
            for cb in range(D // CBW):
                fb = cb % 2
                S.dma("sp", lambda e, fb=fb, cb=cb: e.dma_start(out=FG[fb][:], in_=fg_d[:, cb * CBW:(cb + 1) * CBW]), w=[FGb[fb]])
                for ti, (t0, rows) in enumerate(OT_TILES):
                    b = n % 2
                    n += 1
                    S.dma("sp", lambda e, b=b, cb=cb: e.dma_start(out=YF[b][0:rows, :], in_=y_d[p, t0:t0 + rows, cb * CBW:(cb + 1) * CBW]),
                          r=[ydb[ti]], w=[YFb[b]], sembuf=YFb[b])
                    S.op("dve", lambda e, b=b, ti=ti, fb=fb: e.scalar_tensor_tensor(
                        out=YF[b][0:rows, :], in0=YF[b][0:rows, :], scalar=R2[0:rows, 10 + ti:11 + ti], in1=FG[fb][0:rows, :],
                        op0=ALU.mult, op1=ALU.mult), r=[YFb[b], R2b, FGb[fb]], w=[YFb[b]])
                    S.dma("act", lambda e, b=b, cb=cb: e.dma_start(out=y_d[p, t0:t0 + rows, cb * CBW:(cb + 1) * CBW], in_=YF[b][0:rows, :]),
                          r=[YFb[b]], sembuf=YFb[b])
                    yield

        FGEN = [None]
        FPEND = [None]

        def do_fpull(n):
            for _ in range(n):
                if FGEN[0] is None:
                    return
                try:
                    next(FGEN[0])
                except StopIteration:
                    FGEN[0] = None
                    return


        def final_sb(p, YL, YLb, A):
            FG = [T("FGs%d" % i, [128, 512], F32, A) for i in range(2)]
            FGb = [Buf("FGs0"), Buf("FGs1")]
            for ti, (t0, rows) in enumerate(OT_TILES):
                S.op("dve", lambda e, ti=ti: e.tensor_reduce(out=R2[0:rows, ti:ti + 1], in_=SSQ[p][0:rows, ti, :], axis=AX.X, op=ALU.add),
                     r=[SSQb[p]], w=[R2b])
                S.op("act", lambda e, ti=ti: e.activation(out=R2[0:rows, 5 + ti:6 + ti], in_=R2[0:rows, ti:ti + 1], func=AF.Sqrt,
                                                          scale=1.0 / D, bias=EPS), r=[R2b], w=[R2b])
                S.op("dve", lambda e, ti=ti: e.reciprocal(out=R2[0:rows, 10 + ti:11 + ti], in_=R2[0:rows, 5 + ti:6 + ti]),
                     r=[R2b], w=[R2b])
            n = 0
            for ti, (t0, rows) in enumerate(OT_TILES):
                for cb in range(8):
                    fb = n % 2
                    n += 1
                    S.dma("sp", lambda e, fb=fb, cb=cb: e.dma_start(out=FG[fb][:], in_=fg_d[:, cb * 512:(cb + 1) * 512]), w=[FGb[fb]])
                    yv = YL[ti][0:rows, cb * 512:(cb + 1) * 512]
                    S.op("dve", lambda e, yv=yv, ti=ti, fb=fb: e.scalar_tensor_tensor(
                        out=yv, in0=yv, scalar=R2[0:rows, 10 + ti:11 + ti], in1=FG[fb][0:rows, :],
                        op0=ALU.mult, op1=ALU.mult), r=[YLb[ti], R2b, FGb[fb]], w=[YLb[ti]])
                S.dma("act" if ti % 2 else "sp", lambda e, ti=ti: e.dma_start(out=y_d[p, t0:t0 + rows, :], in_=YL[ti][0:rows, :]),
                      r=[YLb[ti]])

        GEN = [None]

        def do_pull(n):
            for _ in range(n):
                if GEN[0] is None:
                    return
                try:
                    next(GEN[0])
                except StopIteration:
                    GEN[0] = None
                    return

        HM = CC[:, :, :].rearrange("p a b -> p (a b)")[:, 0:8192].rearrange("p (k n) -> p k n", k=32)
        HMb = Buf("HM")

        def kv_proj(A):
            KST = T("KST", [128, 2, 128], F32, A)
            KSTb = Buf("KST")
            for which_ in range(2):
                for i in range(4):
                    view, sb, sl = w_get("w_mk" if which_ == 0 else "w_mv", i * 256)
                    for ci in range(2):
                        c = 2 * i + ci
                        t, tb = pq()
                        for kc in range(32):
                            S.op("pe", lambda e, kc=kc, t=t, view=view, ci=ci: e.matmul(
                                t[:, 0:256], lhsT=view[:, kc, ci * 128:(ci + 1) * 128], rhs=HM[:, kc, :],
                                start=(kc == 0), stop=(kc == 31)), r=[sb, HMb] + CCb[12:16], w=[tb])
                        if which_ == 0:
                            S.op("act", lambda e, c=c, t=t: e.copy(out=KT[:, c, :], in_=t[:, 0:256]), r=[tb], w=[KTb])
                        hi, lo, hlb = split(t[:, 0:256], tb, 128, 256)
                        t2, t2b = pq()
                        for mc in range(2):
                            tr32(t2[:, mc * 128:(mc + 1) * 128], t2b, hi, lo, hlb, 128, mc * 128, (mc + 1) * 128)
                        S.op("act", lambda e, t2=t2: e.copy(out=KST[:], in_=t2[:, 0:256].rearrange("p (m n) -> p m n", m=2)),
                             r=[t2b], w=[KSTb])
                        if which_ == 1:
                            S.op("act", lambda e, c=c: e.copy(out=VB[:, :, c * 128:(c + 1) * 128], in_=KST[:]), r=[KSTb], w=[VBb])
                        od = mk_d if which_ == 0 else mv_d
                        S.dma("act", lambda e, c=c, od=od: e.dma_start(
                            out=od.rearrange("(mc q) c -> q mc c", q=128)[:, :, c * 128:(c + 1) * 128], in_=KST[:]),
                            r=[KSTb], sembuf=KSTb)
                        yield
                    w_release(sl)

        def run(which, p):
            if which == "p0":
                with contextlib.ExitStack() as A:
                    norm_transpose(lambda r0, rows, p=p: x_d[p, r0:r0 + rows, :], XT_TILES, C_G, HT, HTb, A, "x", npull=0)
                    S.barrier()

            if which == "kv" and "kv" in phases:
                with contextlib.ExitStack() as A2:
                    norm_transpose(lambda r0, rows: mem_d[r0:r0 + rows, :], [(0, 128), (128, 128)], C_MG, HM, HMb, A2, "m", nbuf=1)
                    S.barrier()
                S.deps("pool", [HMb], [])
                w_fill()

            if which == "a" and "a" in phases:
                with contextlib.ExitStack() as A:
                    PL = T("PL", [128, 12, W], BF16, A)
                    if FPEND[0] is not None:
                        FGEN[0] = out_final(FPEND[0], A)
                        FPEND[0] = None
                    WP = T("WP", [128, 4, 3, 384], BF16, A)
                    WPb = Buf("WP")
                    for g in range(4):
                        S.dma("pool", lambda e, g=g: e.dma_start(
                            out=WP[:, g], in_=w_pool_d[g].rearrange("(cc p) d -> p cc d", p=128)), wa=[WPb], sembuf=WPb)
                    EU = [T("EU%d" % i, [128, W], F32, A) for i in range(2)]
                    ES = [T("ES%d" % i, [128, 8, 19], F32, A) for i in range(2)]
                    TW = [T("TW%d" % i, [128, W], F32, A) for i in range(2)]
                    TS = [T("TS%d" % i, [128, 8, 19], F32, A) for i in range(2)]
                    t16 = T("t16", [128, 16], F32, A)
                    SGt = [T("SGt%d" % i, [128, W], F32, A) for i in range(2)]
                    UT = T("UT", [128, 12, 15], F32, A)
                    USA = T("USA", [128, 12, 32], F32, A)
                    STJ = [T("STJ%d" % i, [128, 128], F32, A) for i in range(2)]
                    STG = [T("STG%d" % i, [32, 512], F32, A) for i in range(2)]
                    PLb = [Buf("PL%d" % j) for j in range(12)]
                    EUb, ESb = [Buf("EU0"), Buf("EU1")], [Buf("ES0"), Buf("ES1")]
                    TWb, TSb = [Buf("TW0"), Buf("TW1")], [Buf("TS0"), Buf("TS1")]
                    t16b, UTb, USAb = Buf("t16"), Buf("UT"), Buf("USA")
                    SGtb = [Buf("SGt0"), Buf("SGt1")]
                    STJb = [Buf("STJ0"), Buf("STJ1")]
                    STGb = [Buf("STG0"), Buf("STG1")]
                    pssb = Buf("pss")
                    spv = spin_d[8 * p:8 * p + 8].rearrange("s r c -> (s r) c")
                    S.dma("sp", lambda e: e.dma_start(
                        out=pss_d[8 * p:8 * p + 8, 0:11, :].rearrange("s r c -> s (r c)"),
                        in_=spin_d[8 * p:8 * p + 8, 4:15, :].rearrange("s r c -> s (r c)")), wa=[pssb], sembuf=pssb)
                    for i in range(6):
                        view, sb, sl = w_get("w_in", COL_U + i * 256)
                        for ci in range(2):
                            j = 2 * i + ci
                            b = j % 2
                            g = j // 3
                            wdt = POOLW[g]
                            S.dma("sp", lambda e, b=b, j=j: e.dma_start(out=STJ[b][0:120, :], in_=spv[:, j * 128:(j + 1) * 128]),
                                  w=[STJb[b]])
                            t, tb = proj(view, sb, ci, 32, 0, HT, HTb, NT)
                            S.op("act", lambda e, b=b, t=t: e.copy(out=EU[b][:], in_=t[:, 0:W]), r=[tb], w=[EUb[b]])
                            S.op("act", lambda e, j=j, t=t: e.copy(out=USA[:, j, :], in_=t[:, W:NT]), r=[tb], w=[USAb])
                            t2, t2b = pz()
                            hi, lo, hlb = split(STJ[b][0:120, :], STJb[b], 120, 128)
                            tr32(t2[:, 0:120], t2b, hi, lo, hlb, 120, 0, 128)
                            S.op("dve", lambda e, b=b, t2=t2: e.tensor_copy(
                                out=ES[b][:, :, 0:15], in_=t2[:, 0:120].rearrange("p (s r) -> p s r", s=8)), r=[t2b], w=[ESb[b]])
                            S.op("dve", lambda e, b=b, j=j: e.tensor_copy(
                                out=ES[b][:, :, 15:19], in_=USA[:, j, :].rearrange("p (s r) -> p s r", s=8)), r=[USAb], w=[ESb[b]])
                            S.op("dve", lambda e, b=b, j=j: e.tensor_copy(out=UT[:, j, :], in_=EU[b][:, W - 15:W]), r=[EUb[b]], w=[UTb])
                            src, srcb, ssrc, ssrcb = EU[b], EUb[b], ES[b], ESb[b]
                            for l in range(1, g + 2):
                                off = 1 << (l - 1)
                                lo = (1 << l) - 1
                                d_, db_ = TW[l % 2], TWb[l % 2]
                                S.op("dve", lambda e, d_=d_, src=src, lo=lo, off=off: e.tensor_tensor(
                                    out=d_[:, lo:W], in0=src[:, lo:W], in1=src[:, lo - off:W - off], op=ALU.add),
                                    r=[srcb], w=[db_])
                                ds_, dsb_ = TS[l % 2], TSb[l % 2]
                                S.op("dve", lambda e, ds_=ds_, ssrc=ssrc, lo=lo, off=off: e.tensor_tensor(
                                    out=ds_[:, :, lo:19], in0=ssrc[:, :, lo:19], in1=ssrc[:, :, lo - off:19 - off], op=ALU.add),
                                    r=[ssrcb], w=[dsb_])
                                src, srcb, ssrc, ssrcb = d_, db_, ds_, dsb_
                            S.op("dve", lambda e, j=j, src=src, b=b, wdt=wdt: e.scalar_tensor_tensor(
                                out=PL[:, j, 16:512], in0=src[:, 48:W], scalar=1.0 / wdt, in1=EU[b][:, 48:W],
                                op0=ALU.mult, op1=ALU.subtract), r=[srcb, EUb[b]], w=[PLb[j]])
                            ic = C_INV + (p * 4 + g) * 16
                            S.op("dve", lambda e, src=src, ic=ic: e.tensor_tensor(
                                out=t16[:], in0=src[:, 32:48], in1=CS[:, ic:ic + 16], op=ALU.mult), r=[srcb, CSb], w=[t16b])
                            S.op("dve", lambda e, j=j, b=b: e.tensor_tensor(
                                out=PL[:, j, 0:16], in0=t16[:], in1=EU[b][:, 32:48], op=ALU.subtract),
                                r=[t16b, EUb[b]], w=[PLb[j]])
                            S.op("dve", lambda e, j=j, ssrc=ssrc, b=b, wdt=wdt: e.scalar_tensor_tensor(
                                out=PL[:, j, 512:W].rearrange("p (s r) -> p s r", s=8), in0=ssrc[:, :, 15:19],
                                scalar=1.0 / wdt, in1=ES[b][:, :, 15:19], op0=ALU.mult, op1=ALU.subtract),
                                r=[ssrcb, ESb[b]], w=[PLb[j]])
                            do_fpull(2)
                        w_release(sl)
                    for q in range(3):
                        t, tb = pz()
                        hi, lo, hlb = split(UT[:, 4 * q:4 * q + 4, :].rearrange("p a b -> p (a b)"), UTb, 128, 60)
                        for k in range(4):
                            tr32(t[0:15, k * 128:(k + 1) * 128], tb, hi, lo, hlb, 128, k * 15, (k + 1) * 15)
                        hi, lo, hlb = split(USA[:, 4 * q:4 * q + 4, :].rearrange("p a b -> p (a b)"), USAb, 128, 128)
                        for k in range(4):
                            tr32(t[0:32, 512 + k * 128:512 + (k + 1) * 128], tb, hi, lo, hlb, 128, k * 32, (k + 1) * 32)
                        S.op("act", lambda e, t=t: e.copy(out=STG[0][0:15, :], in_=t[0:15, 0:512]), r=[tb], w=[STGb[0]])
                        S.op("act", lambda e, t=t: e.copy(out=STG[1][0:32, :], in_=t[0:32, 512:1024]), r=[tb], w=[STGb[1]])
                        S.dma("act", lambda e, q=q: e.dma_start(out=psp_d[p, :, q * 512:(q + 1) * 512], in_=STG[0][0:15, :]),
                              r=[STGb[0]], sembuf=STGb[0])
                        for s in range(8):
                            S.dma("act", lambda e, q=q, s=s: e.dma_start(
                                out=pss_d[8 * p + s, 11:15, q * 512:(q + 1) * 512], in_=STG[1][4 * s:4 * s + 4, :]),
                                r=[STGb[1]], sembuf=STGb[1])
                    for i in range(6):
                        view, sb, sl = w_get("w_in", COL_GA + i * 256)
                        for ci in range(2):
                            j = 2 * i + ci
                            b = j % 2
                            g, dj = j // 3, j % 3
                            t, tb = proj(view, sb, ci, 32, 32, HT, HTb, NT)
                            S.op("act", lambda e, b=b, t=t: e.activation(out=SGt[b][:], in_=t[:, 0:W], func=AF.Silu),
                                 r=[tb], w=[SGtb[b]])
                            m, mb = pz()
                            for cc in range(3):
                                for (a, bb) in ((0, 512), (512, W)):
                                    S.op("pe", lambda e, cc=cc, a=a, bb=bb, g=g, dj=dj, m=m: e.matmul(
                                        m[:, a:bb], lhsT=WP[:, g, cc, dj * 128:(dj + 1) * 128], rhs=PL[:, 3 * g + cc, a:bb],
                                        start=(cc == 0), stop=(cc == 2)), r=[WPb, PLb[3 * g + cc]], w=[mb])
                            S.op("dve", lambda e, j=j, b=b, m=m: e.scalar_tensor_tensor(
                                out=CC[:, j, :], in0=m[:, 0:W], scalar=col(C_PSC + j), in1=SGt[b][:],
                                op0=ALU.mult, op1=ALU.mult), r=[mb, SGtb[b], CSb], w=[CCb[j]])
                            do_fpull(2)
                        w_release(sl)
                    do_fpull(1000)
                    S.barrier()

            if which == "b" and "b" in phases:
                with contextlib.ExitStack() as A:
                    YB = T("YB", [128, 12, W], F32, A)
                    if p == 0 and "kv" in phases:
                        GEN[0] = kv_proj(A)
                    EA = [T("EA%d" % i, [128, W], F32, A) for i in range(2)]
                    ESA = [T("ESA%d" % i, [128, 8, 34], F32, A) for i in range(2)]
                    SG = [T("SG%d" % i, [128, NT], F32, A) for i in range(2)]
                    SQ = T("SQ", [128, W], F32, A)
                    SS = T("SSm", [128, W], F32, A)
                    QS = T("QSm", [128, W], F32, A)
                    MU = T("MU", [128, W], F32, A)
                    RS = T("RS", [128, W], F32, A)
                    Y1 = MU[:, 0:512]
                    AT = T("AT", [128, 12, 30], F32, A)
                    ASA = T("ASA", [128, 12, 32], F32, A)
                    STJ = [T("STJc%d" % i, [128, 2, 128], F32, A) for i in range(2)]
                    STG = [SG[i][0:32, 0:512] for i in range(2)]

                    def lnloc(kc):
                        if kc < 8:
                            return CC[:, 24 + kc, :], CCb[24 + kc]
                        q_ = kc - 8
                        return EA[q_ // 2].bitcast(BF16)[:, (q_ % 2) * W:(q_ % 2 + 1) * W], LNXb[q_]
                    YBb = [Buf("YB%d" % j) for j in range(12)]
                    EAb, ESAb = [Buf("EA0"), Buf("EA1")], [Buf("ESA0"), Buf("ESA1")]
                    LNXb = [EAb[0], EAb[0], EAb[1], EAb[1]]
                    SGb = [Buf("SG0"), Buf("SG1")]
                    YS1b, SQb, SSb, QSb, MUb, RSb = Buf("YS1"), Buf("SQ"), Buf("SS"), Buf("QS"), Buf("MU"), Buf("RS")
                    Y1b = MUb
                    ATb, ASAb = Buf("AT"), Buf("ASA")
                    STJb = [Buf("STJc0"), Buf("STJc1")]
                    STGb = SGb
                    cssb = Buf("css")
                    scv = scin_d[8 * p:8 * p + 8].rearrange("s r c -> (s r) c")
                    S.dma("sp", lambda e: e.dma_start(
                        out=css_d[8 * p:8 * p + 8, 0:26, :].rearrange("s r c -> s (r c)"),
                        in_=scin_d[8 * p:8 * p + 8, 4:30, :].rearrange("s r c -> s (r c)")), wa=[cssb], sembuf=cssb)
                    for i in range(6):
                        for ci in range(2):
                            j = 2 * i + ci
                            b = j % 2
                            vgv, vgsb, vgsl = w_get("w_in_vg", j)
                            for mc in range(2):
                                S.dma("sp", lambda e, b=b, j=j, mc=mc: e.dma_start(
                                    out=STJ[b][0:120, mc, :], in_=scv[mc * 120:(mc + 1) * 120, j * 128:(j + 1) * 128]),
                                    wa=[STJb[b]] if mc else (), w=() if mc else [STJb[b]], sembuf=STJb[b])
                            tv, tvb = proj(vgv, vgsb, 0, 32, 0, HT, HTb, NT)
                            tg, tgb = proj(vgv, vgsb, 1, 32, 0, HT, HTb, NT)
                            w_release(vgsl)
                            S.op("act", lambda e, b=b, tg=tg: e.activation(out=SG[b][:], in_=tg[:, 0:NT], func=AF.Sigmoid),
                                 r=[tgb], w=[SGb[b]])
                            S.op("dve", lambda e, b=b, tv=tv: e.tensor_tensor(out=EA[b][:], in0=tv[:, 0:W], in1=SG[b][:, 0:W],
                                                                              op=ALU.mult), r=[tvb, SGb[b]], w=[EAb[b]])
                            S.op("dve", lambda e, b=b, j=j, tv=tv: e.tensor_tensor(out=ASA[:, j, :], in0=tv[:, W:NT], in1=SG[b][:, W:NT],
                                                                                   op=ALU.mult), r=[tvb, SGb[b]], w=[ASAb])
                            t2, t2b = pz()
                            hi, lo, hlb = split(STJ[b][0:120, :, :].rearrange("p a b -> p (a b)"), STJb[b], 120, 256)
                            for mc in range(2):
                                tr32(t2[:, mc * 120:(mc + 1) * 120], t2b, hi, lo, hlb, 120, mc * 128, (mc + 1) * 128)
                            S.op("act", lambda e, b=b, t2=t2: e.copy(
                                out=ESA[b][:, :, 0:30], in_=t2[:, 0:240].rearrange("p (s r) -> p s r", s=8)), r=[t2b], w=[ESAb[b]])
                            S.op("dve", lambda e, b=b, j=j: e.tensor_copy(
                                out=ESA[b][:, :, 30:34], in_=ASA[:, j, :].rearrange("p (s r) -> p s r", s=8)), r=[ASAb], w=[ESAb[b]])
                            S.op("dve", lambda e, b=b, j=j: e.tensor_copy(out=AT[:, j, :], in_=EA[b][:, W - 30:W]), r=[EAb[b]], w=[ATb])
                            wc = C_WDW + j * 31
                            ysv = YB[:, j, 512:W].rearrange("p (s r) -> p s r", s=8)
                            S.op("dve", lambda e, b=b, j=j, wc=wc: e.tensor_scalar(
                                out=YB[:, j, 0:512], in0=EA[b][:, 2:514], scalar1=col(wc), scalar2=col(C_BDW + j),
                                op0=ALU.mult, op1=ALU.add), r=[EAb[b], CSb], w=[YBb[j]])
                            S.op("dve", lambda e, b=b, wc=wc: e.tensor_scalar(
                                out=Y1[:], in0=EA[b][:, 3:515], scalar1=col(wc + 1), scalar2=None, op0=ALU.mult),
                                r=[EAb[b], CSb], w=[Y1b])
                            for k in range(2, 31):
                                if k % 5 == 0 and (p > 0 or k == 5):
                                    do_pull(1)
                                if k % 2 == 0:
                                    S.op("dve", lambda e, b=b, j=j, k=k, wc=wc: e.scalar_tensor_tensor(
                                        out=YB[:, j, 0:512], in0=EA[b][:, 2 + k:514 + k], scalar=col(wc + k), in1=YB[:, j, 0:512],
                                        op0=ALU.mult, op1=ALU.add), r=[EAb[b], CSb, YBb[j]], w=[YBb[j]])
                                else:
                                    S.op("dve", lambda e, b=b, k=k, wc=wc: e.scalar_tensor_tensor(
                                        out=Y1[:], in0=EA[b][:, 2 + k:514 + k], scalar=col(wc + k), in1=Y1[:],
                                        op0=ALU.mult, op1=ALU.add), r=[EAb[b], CSb, Y1b], w=[Y1b])
                            S.op("dve", lambda e, j=j: e.tensor_tensor(out=YB[:, j, 0:512], in0=YB[:, j, 0:512], in1=Y1[:], op=ALU.add),
                                 r=[YBb[j], Y1b], w=[YBb[j]])
                            tmpv = SQ[:, 0:248].rearrange("p (s k) -> p s k", s=8)
                            for t_ in range(4):
                                S.op("dve", lambda e, b=b, wc=wc, t_=t_, tmpv=tmpv: e.tensor_tensor(
                                    out=tmpv, in0=ESA[b][:, :, t_:t_ + 31],
                                    in1=CS[:, wc:wc + 31][:, None, :].broadcast_to([128, 8, 31]), op=ALU.mult),
                                    r=[ESAb[b], CSb], w=[SQb])
                                S.op("dve", lambda e, ysv=ysv, t_=t_, tmpv=tmpv: e.tensor_reduce(
                                    out=ysv[:, :, t_:t_ + 1], in_=tmpv, axis=AX.X, op=ALU.add), r=[SQb], w=[YBb[j]])
                            S.op("dve", lambda e, ysv=ysv, j=j: e.tensor_scalar(out=ysv, in0=ysv, scalar1=col(C_BDW + j), scalar2=None,
                                                                              op0=ALU.add), r=[YBb[j], CSb], w=[YBb[j]])
                            S.op("act", lambda e, j=j: e.activation(out=SQ[:], in_=YB[:, j, :], func=AF.Square), r=[YBb[j]], w=[SQb])
                            if j == 0:
                                S.op("dve", lambda e, j=j: e.tensor_copy(out=SS[:], in_=YB[:, j, :]), r=[YBb[j]], w=[SSb])
                                S.op("dve", lambda e: e.tensor_copy(out=QS[:], in_=SQ[:]), r=[SQb], w=[QSb])
                            else:
                                S.op("dve", lambda e, j=j: e.tensor_tensor(out=SS[:], in0=SS[:], in1=YB[:, j, :], op=ALU.add),
                                     r=[YBb[j], SSb], w=[SSb])
                                S.op("dve", lambda e: e.tensor_tensor(out=QS[:], in0=QS[:], in1=SQ[:], op=ALU.add),
                                     r=[SQb, QSb], w=[QSb])
                    do_pull(8)
                    for q in range(3):
                        t, tb = pz()
                        hi, lo, hlb = split(AT[:, 4 * q:4 * q + 4, :].rearrange("p a b -> p (a b)"), ATb, 128, 120)
                        for k in range(4):
                            tr32(t[0:30, k * 128:(k + 1) * 128], tb, hi, lo, hlb, 128, k * 30, (k + 1) * 30)
                        hi, lo, hlb = split(ASA[:, 4 * q:4 * q + 4, :].rearrange("p a b -> p (a b)"), ASAb, 128, 128)
                        for k in range(4):
                            tr32(t[0:32, 512 + k * 128:512 + (k + 1) * 128], tb, hi, lo, hlb, 128, k * 32, (k + 1) * 32)
                        S.op("act", lambda e, t=t: e.copy(out=STG[0][0:30, :], in_=t[0:30, 0:512]), r=[tb], w=[STGb[0]])
                        S.op("act", lambda e, t=t: e.copy(out=STG[1][0:32, :], in_=t[0:32, 512:1024]), r=[tb], w=[STGb[1]])
                        S.dma("act", lambda e, q=q: e.dma_start(out=csp_d[p, :, q * 512:(q + 1) * 512], in_=STG[0][0:30, :]),
                              r=[STGb[0]], sembuf=STGb[0])
                        for s in range(8):
                            S.dma("act", lambda e, q=q, s=s: e.dma_start(
                                out=css_d[8 * p + s, 26:30, q * 512:(q + 1) * 512], in_=STG[1][4 * s:4 * s + 4, :]),
                                r=[STGb[1]], sembuf=STGb[1])
                    ta, tab = pz()
                    tq, tqb = pz()
                    for (src_, srcb_, dst_, dstb_) in ((SS, SSb, ta, tab), (QS, QSb, tq, tqb)):
                        hi, lo, hlb = split(src_[:, :], srcb_, 128, W)
                        for (a, bb) in ((0, 512), (512, W)):
                            S.op("pe", lambda e, a=a, bb=bb, dst_=dst_, hi=hi: e.matmul(dst_[:, a:bb], lhsT=ONES[:], rhs=hi[:, a:bb],
                                                                                      start=True, stop=False), r=[IDb, hlb], w=[dstb_])
                            S.op("pe", lambda e, a=a, bb=bb, dst_=dst_, lo=lo: e.matmul(dst_[:, a:bb], lhsT=ONES[:], rhs=lo[:, a:bb],
                                                                                      start=False, stop=True), r=[IDb, hlb], w=[dstb_])
                    S.op("dve", lambda e: e.tensor_scalar(out=MU[:], in0=ta[:, 0:W], scalar1=1.0 / 1536, scalar2=None, op0=ALU.mult),
                         r=[tab], w=[MUb])
                    S.op("dve", lambda e: e.tensor_tensor(out=SQ[:], in0=MU[:], in1=MU[:], op=ALU.mult), r=[MUb], w=[SQb])
                    S.op("dve", lambda e: e.scalar_tensor_tensor(out=RS[:], in0=tq[:, 0:W], scalar=1.0 / 1536, in1=SQ[:],
                                                                  op0=ALU.mult, op1=ALU.subtract), r=[tqb, SQb], w=[RSb])
                    S.op("act", lambda e: e.activation(out=RS[:], in_=RS[:], func=AF.Sqrt, scale=1.0, bias=EPS), r=[RSb], w=[RSb])
                    S.op("dve", lambda e: e.reciprocal(out=RS[:], in_=RS[:]), r=[RSb], w=[RSb])
                    TT, TTb = [SS, QS], [SSb, QSb]
                    for j in range(12):
                        b = j % 2
                        S.op("dve", lambda e, j=j, b=b: e.tensor_tensor(out=TT[b][:], in0=YB[:, j, :], in1=MU[:], op=ALU.subtract),
                             r=[YBb[j], MUb], w=[TTb[b]])
                        S.op("dve", lambda e, b=b: e.tensor_tensor(out=TT[b][:], in0=TT[b][:], in1=RS[:], op=ALU.mult),
                             r=[TTb[b], RSb], w=[TTb[b]])
                        lo_, lob_ = lnloc(j)
                        S.op("act", lambda e, j=j, b=b, lo_=lo_: e.activation(out=lo_, in_=TT[b][:], func=AF.Silu,
                                                                     scale=col(C_LNG + j), bias=col(C_LNB + j)),
                             r=[TTb[b], CSb], w=[lob_])
                    for i in range(6):
                        pv, psb, psl = w_get("w_pw", i * 256, 256, 12)
                        gv, gsb, gsl = w_get("w_in", COL_GB + i * 256)
                        for ci in range(2):
                            dj = 2 * i + ci
                            b = dj % 2
                            tg, tgb = proj(gv, gsb, ci, 32, 32, HT, HTb, NT)
                            S.op("act", lambda e, b=b, tg=tg: e.activation(out=SG[b][:, 0:W], in_=tg[:, 0:W], func=AF.Silu),
                                 r=[tgb], w=[SGb[b]])
                        for ci in range(2):
                            dj = 2 * i + ci
                            b = dj % 2
                            o, ob = pz()
                            for kc in range(12):
                                for (a, bb) in ((0, 512), (512, W)):
                                    lo_, lob_ = lnloc(kc)
                                    S.op("pe", lambda e, kc=kc, a=a, bb=bb, ci=ci, o=o, pv=pv, lo_=lo_: e.matmul(
                                        o[:, a:bb], lhsT=pv[:, kc, ci * 128:(ci + 1) * 128], rhs=lo_[:, a:bb],
                                        start=(kc == 0), stop=(kc == 11)), r=[psb, lob_], w=[ob])
                            S.op("dve", lambda e, b=b, dj=dj, o=o: e.tensor_tensor(
                                out=CC[:, 12 + dj, :], in0=o[:, 0:W], in1=SG[b][:, 0:W], op=ALU.mult),
                                r=[ob, SGb[b]], w=[CCb[12 + dj]])
                        w_release(psl)
                        w_release(gsl)
                    S.barrier()

            if which == "c" and "c" in phases:
                with contextlib.ExitStack() as A:
                    QT = T("QT", [128, 8, W], BF16, A)
                    PT = T("PT", [128, 8, W], BF16, A)
                    QM = T("QM", [128, 8, 8, 32], BF16, A)
                    Pf = [T("Pf0", [128, 4, 256], F32, A)] * 2
                    Pb = [T("Pb0", [128, 4, 256], BF16, A)] * 2
                    SM = [T("SM%d" % i, [128, 16], F32, A) for i in range(2)]
                    KB = [T("KB%d" % i, [128, 2, 1024], BF16, A) for i in range(2)]
                    VS = [T("VS%d" % i, [128, 2, 1024], BF16, A) for i in range(2)]
                    KTs = [T("KTs0", [128, 8, 256], BF16, A)] * 2
                    SGC = [T("SGC%d" % i, [128, W], F32, A) for i in range(2)]
                    OSs = T("OSs", [128, 8, 32], F32, A)
                    QTb, PTb, QMb, OSsb = Buf("QT"), Buf("PT"), Buf("QM"), Buf("OSs")
                    Pfb, Pbb, SMb = [Buf("Pf0")] * 2, [Buf("Pb0")] * 2, [Buf("SM0"), Buf("SM1")]
                    KBb, VSb, KTsb = [Buf("KB0"), Buf("KB1")], [Buf("VS0"), Buf("VS1")], [Buf("KTs0")] * 2
                    SGCb = [Buf("SGC0"), Buf("SGC1")]
                    for i in range(4):
                        view, sb, sl = w_get("w_in", COL_Q + i * 256)
                        for ci in range(2):
                            c = 2 * i + ci
                            t, tb = proj(view, sb, ci, 32, 32, HT, HTb, NT)
                            S.op("act", lambda e, c=c, t=t: e.copy(out=QT[:, c, :], in_=t[:, 0:W]), r=[tb], w=[QTb])
                        w_release(sl)
                    S.op("dve", lambda e: e.memset(QM[:], 0.0), w=[QMb])
                    for s in range(8):
                        S.op("dve", lambda e, s=s: e.tensor_copy(out=QM[:, s, :, 4 * s:4 * s + 4],
                                                                 in_=QT[:, :, 512 + 4 * s:516 + 4 * s]), r=[QTb], w=[QMb])
                    sm_n = [0]

                    def softmax(heads, rows):
                        b = sm_n[0] % 2
                        sm_n[0] += 1
                        sm = SM[b]
                        for h, (hap, hb) in enumerate(heads):
                            S.op("dve", lambda e, h=h, hap=hap: e.tensor_reduce(
                                out=sm[0:rows, h:h + 1], in_=hap, axis=AX.X, op=ALU.max), r=[hb], w=[SMb[b]])
                        S.op("dve", lambda e: e.tensor_scalar(out=sm[0:rows, 4:8], in0=sm[0:rows, 0:4], scalar1=-1.0 / 16, scalar2=None,
                                                               op0=ALU.mult), r=[SMb[b]], w=[SMb[b]])
                        for h, (hap, hb) in enumerate(heads):
                            S.op("act", lambda e, h=h, hap=hap: e.activation(
                                out=Pf[b][0:rows, h, :], in_=hap, func=AF.Exp, scale=1.0 / 16,
                                bias=sm[0:rows, 4 + h:5 + h], accum_out=sm[0:rows, 8 + h:9 + h]),
                                r=[hb, SMb[b]], w=[Pfb[b], SMb[b]])
                        S.op("dve", lambda e: e.reciprocal(out=sm[0:rows, 12:16], in_=sm[0:rows, 8:12]), r=[SMb[b]], w=[SMb[b]])
                        S.op("dve", lambda e: e.tensor_tensor(
                            out=Pb[b][0:rows], in0=Pf[b][0:rows], in1=sm[0:rows, 12:16][:, :, None].broadcast_to([rows, 4, 256]),
                            op=ALU.mult), r=[Pfb[b], SMb[b]], w=[Pbb[b]])
                        return b

                    def p_transpose(b, rows, c0):
                        t, tb = pb()
                        tv = t[:, :].rearrange("p (k n) -> p k n", k=8)
                        for h in range(4):
                            for mc in range(2):
                                S.op("pe", lambda e, h=h, mc=mc: e.transpose(
                                    out=tv[:, 2 * h + mc, 0:rows], in_=Pb[b][0:rows, h, mc * 128:(mc + 1) * 128],
                                    identity=IDB[0:rows, 0:rows]), r=[Pbb[b], IDb], w=[tb])
                        S.op("act", lambda e: e.copy(out=PT[:, :, c0:c0 + rows], in_=tv[:, :, 0:rows]), r=[tb], w=[PTb])

                    for ti in range(4):
                        t, tb = pz()
                        for h in range(4):
                            for ec in range(2):
                                S.op("pe", lambda e, h=h, ec=ec, ti=ti, t=t: e.matmul(
                                    t[:, h * 256:(h + 1) * 256], lhsT=QT[:, 2 * h + ec, ti * 128:(ti + 1) * 128],
                                    rhs=KT[:, 2 * h + ec, :], start=(ec == 0), stop=(ec == 1)), r=[QTb, KTb], w=[tb])
                        b = softmax([(t[:, h * 256:(h + 1) * 256], tb) for h in range(4)], 128)
                        p_transpose(b, 128, ti * 128)
                    tS, tSb = pz()
                    tS2, tS2b = pz()
                    shead = [((tS if h < 2 else tS2)[0:32, (h % 2) * 512:(h % 2) * 512 + 256], (tSb if h < 2 else tS2b)) for h in range(4)]
                    for s in range(8):
                        b = s % 2
                        sg = 8 * p + s
                        S.dma("pool", lambda e, b=b, sg=sg: e.dma_start(
                            out=KB[b][:], in_=ck_d[sg].rearrange("(mc q) c -> q mc c", q=128)), w=[KBb[b]])
                        for half in range(2):
                            t, tb = pb()
                            tv = t[:, :].rearrange("p (k n) -> p k n", k=4)
                            for cq in range(4):
                                c = half * 4 + cq
                                for mc in range(2):
                                    S.op("pe", lambda e, cq=cq, c=c, mc=mc, b=b: e.transpose(
                                        out=tv[:, cq, mc * 128:(mc + 1) * 128], in_=KB[b][:, mc, c * 128:(c + 1) * 128],
                                        identity=IDB[:]), r=[KBb[b], IDb], w=[tb])
                            S.op("act" if half else "dve",
                                 (lambda e, b=b, half=half, tv=tv: e.copy(out=KTs[b][:, half * 4:half * 4 + 4, :], in_=tv)) if half else
                                 (lambda e, b=b, half=half, tv=tv: e.tensor_copy(out=KTs[b][:, half * 4:half * 4 + 4, :], in_=tv)),
                                 r=[tb], w=[KTsb[b]])
                        for h in range(4):
                            for ec in range(2):
                                S.op("pe", lambda e, h=h, ec=ec, s=s, b=b: e.matmul(
                                    shead[h][0], lhsT=QM[:, s, 2 * h + ec, :], rhs=KTs[b][:, 2 * h + ec, :],
                                    start=(s == 0 and ec == 0), stop=(s == 7 and ec == 1)), r=[QMb, KTsb[b]], w=[shead[h][1]])
                    b = softmax(shead, 32)
                    p_transpose(b, 32, 512)
                    tO, tOb = pz()
                    for s in range(8):
                        b = s % 2
                        sg = 8 * p + s
                        S.dma("pool", lambda e, b=b, sg=sg: e.dma_start(
                            out=VS[b][:], in_=cv_d[sg].rearrange("(mc q) c -> q mc c", q=128)), w=[VSb[b]])
                        for c in range(8):
                            h = c // 2
                            for mc in range(2):
                                S.op("pe", lambda e, c=c, h=h, mc=mc, s=s, b=b: e.matmul(
                                    tO[:, c * 32 + 4 * s:c * 32 + 4 * s + 4], lhsT=VS[b][:, mc, c * 128:(c + 1) * 128],
                                    rhs=PT[:, 2 * h + mc, 512 + 4 * s:516 + 4 * s], start=(mc == 0), stop=(mc == 1)),
                                    r=[VSb[b], PTb], w=[tOb])
                    S.op("act", lambda e: e.copy(out=OSs[:], in_=tO[:, 0:256].rearrange("p (c t) -> p c t", c=8)), r=[tOb], w=[OSsb])
                    for i in range(4):
                        view, sb, sl = w_get("w_in", COL_GC + i * 256)
                        for ci in range(2):
                            c = 2 * i + ci
                            b = c % 2
                            h = c // 2
                            t, tb = proj(view, sb, ci, 32, 32, HT, HTb, NT)
                            S.op("act", lambda e, b=b, t=t: e.activation(out=SGC[b][:], in_=t[:, 0:W], func=AF.Silu), r=[tb], w=[SGCb[b]])
                            o, ob = pz()
                            for mc in range(2):
                                S.op("pe", lambda e, mc=mc, c=c, h=h, o=o: e.matmul(
                                    o[:, 0:512], lhsT=VB[:, mc, c * 128:(c + 1) * 128], rhs=PT[:, 2 * h + mc, 0:512],
                                    start=(mc == 0), stop=(mc == 1)), r=[VBb, PTb], w=[ob])
                            S.op("dve", lambda e, c=c, b=b, o=o: e.tensor_tensor(
                                out=CC[:, 24 + c, 0:512], in0=o[:, 0:512], in1=SGC[b][:, 0:512], op=ALU.mult),
                                r=[ob, SGCb[b]], w=[CCb[24 + c]])
                            S.op("dve", lambda e, c=c, b=b: e.tensor_tensor(
                                out=CC[:, 24 + c, 512:W], in0=OSs[:, c, :], in1=SGC[b][:, 512:W], op=ALU.mult),
                                r=[OSsb, SGCb[b], CCb[24 + c]], w=[CCb[24 + c]])
                        w_release(sl)
                    S.barrier()

        for p in range(NPASS):
            S.new_epoch()
            UMODE[0] = "pz"
            run("p0", p)
            if p == 0:
                run("kv", p)
            UMODE[0] = "pb"
            run("b", p)
            UMODE[0] = "all"
            do_pull(1000)
            if p > 0 and "out" in phases:
                FPEND[0] = p - 1
            run("a", p)
            run("c", p)
            if "out" in phases and p < NPASS - 1:
                GEN[0] = out_main(p)
        if "out" in phases:
            with contextlib.ExitStack() as A:
                p = NPASS - 1
                HTF = HT.bitcast(F32)[:, :, :].rearrange("p a b -> p (a b)")
                YLt = [T("YL%d" % i, [128, D], F32, A) for i in range(3)]
                YL = [YLt[0], YLt[1], YLt[2], HTF[:, 0:D], HTF[:, D:2 * D]]
                YLb = [Buf("YL0"), Buf("YL1"), Buf("YL2"), HTb, HTb]
                GEN[0] = out_main(p, YL, YLb)
                do_pull(1000)
                final_sb(p, YL, YLb, A)
                S.barrier()
        S.barrier(pe_waits=True)
    return nc, rec


def _host_inputs(inp):
    f = lambda a: np.ascontiguousarray(np.asarray(a, dtype=np.float32))
    xp, xs, mem = f(inp["x_prompt"]), f(inp["x_sample"]), f(inp["mem_prompt"])
    ck = f(inp["cache_mem_k"])[0].reshape(128, 256, 1024)
    cv = f(inp["cache_mem_v"])[0].reshape(128, 256, 1024)
    sp = f(inp["state_pool"])[0]
    sc = f(inp["state_conv"])[0]
    shared = {
        "w_in": f(inp["w_in"])[0], "w_mem_k": f(inp["w_mem_k"])[0], "w_mem_v": f(inp["w_mem_v"])[0],
        "w_pool": f(inp["w_pool"])[0], "w_pw": f(inp["w_pw"])[0], "w_out": f(inp["w_out"])[0],
        "ident": np.eye(128, dtype=np.float32),
        "fg": np.ascontiguousarray(np.broadcast_to(f(inp["final_norm_g"])[None, :], (128, D))),
    }
    cbase = np.zeros((128, C_END), np.float32)
    cbase[:, C_G:C_G + 32] = f(inp["norm_g"])[0].reshape(32, 128).T
    cbase[:, C_MG:C_MG + 32] = f(inp["mem_norm_g"])[0].reshape(32, 128).T
    cbase[:, C_PSC:C_PSC + 12] = f(inp["pool_scale"])[0].reshape(12, 128).T
    cbase[:, C_BDW:C_BDW + 12] = f(inp["b_dw"])[0].reshape(12, 128).T
    cbase[:, C_LNG:C_LNG + 12] = f(inp["conv_ln_g"])[0].reshape(12, 128).T
    cbase[:, C_LNB:C_LNB + 12] = f(inp["conv_ln_b"])[0].reshape(12, 128).T
    cbase[:, C_WDW:C_WDW + 372] = f(inp["w_dw"])[0].T.reshape(12, 128, 31).transpose(1, 0, 2).reshape(128, 372)
    in_maps = []
    for c in range(NCORES):
        b, half = c // 2, c % 2
        x = np.zeros((NPASS, NT, D), np.float32)
        cs = cbase.copy()
        for p in range(NPASS):
            s0 = half * 1024 + p * 512
            if s0 > 0:
                x[p, 0:32] = xp[b, s0 - 32:s0]
            x[p, 32:544] = xp[b, s0:s0 + 512]
            x[p, 544:576] = xs[16 * c + 8 * p:16 * c + 8 * p + 8].reshape(32, D)
            for g, wdt in enumerate(POOLW):
                pos = s0 + np.arange(16)
                cs[:, C_INV + (p * 4 + g) * 16:C_INV + (p * 4 + g + 1) * 16] = (
                    1.0 / np.minimum(wdt, pos + 1).astype(np.float32))[None, :]
        m = dict(shared)
        m.update({"x": x, "mem": mem[b], "ck": ck[16 * c:16 * c + 16], "cv": cv[16 * c:16 * c + 16],
                  "sp_in": sp[16 * c:16 * c + 16], "sc_in": sc[16 * c:16 * c + 16], "consts": cs})
        in_maps.append(m)
    return in_maps


_NC_CACHE = {}


def kernel(**inputs):
    in_maps = _host_inputs(inputs)
    if "nc" not in _NC_CACHE:
        _, rec = build_program()
        _NC_CACHE["nc"] = build_program(loads_rec=rec)[0]
    nc = _NC_CACHE["nc"]
    res = run_bass_kernel_spmd(nc, in_maps, core_ids=list(range(NCORES)))
    R = res.results
    y_prompt = np.zeros((4, 2048, D), np.float32)
    y_sample = np.zeros((128, 4, D), np.float32)
    mk = np.zeros((1, 4, 256, 4, 256), np.float32)
    mv = np.zeros((1, 4, 256, 4, 256), np.float32)
    psp = np.zeros((1, 4, 15, 1536), np.float32)
    csp = np.zeros((1, 4, 30, 1536), np.float32)
    pss = np.zeros((1, 128, 15, 1536), np.float32)
    css = np.zeros((1, 128, 30, 1536), np.float32)
    for c in range(NCORES):
        b, half = c // 2, c % 2
        r = R[c]
        for p in range(NPASS):
            s0 = half * 1024 + p * 512
            y_prompt[b, s0:s0 + 512] = r["y"][p, 0:512]
            y_sample[16 * c + 8 * p:16 * c + 8 * p + 8] = r["y"][p, 512:544].reshape(8, 4, D)
        if half == 0:
            mk[0, b] = r["mk"].reshape(256, 4, 256)
            mv[0, b] = r["mv"].reshape(256, 4, 256)
        else:
            psp[0, b] = r["ps_p"][1]
            csp[0, b] = r["cs_p"][1]
        pss[0, 16 * c:16 * c + 16] = r["ps_s"]
        css[0, 16 * c:16 * c + 16] = r["cs_s"]
    return (y_prompt, y_sample, mk, mv, psp, csp, pss, css)
```

```python
import contextlib
import numpy as np
import concourse.bass as bass
import concourse.mybir as mybir
from concourse.bass_utils import run_bass_kernel_spmd

F32 = mybir.dt.float32
BF16 = mybir.dt.bfloat16
AF = mybir.ActivationFunctionType
ALU = mybir.AluOpType
AX = mybir.AxisListType

ENGS = ("pe", "act", "dve", "pool", "sp")
NCORES = 8
D = 4096
DIN = 9728
NT = 576
W = 544
NPASS = 2
EPS = 1e-6
NSLOT = 4
POOLW = (2, 4, 8, 16)
C_G, C_MG, C_PSC, C_BDW, C_LNG, C_LNB, C_WDW, C_INV, C_END = 0, 32, 64, 76, 88, 100, 112, 484, 612
COL_U, COL_GA, COL_V, COL_GG, COL_GB, COL_Q, COL_GC = 0, 1536, 3072, 4608, 6144, 7680, 8704


class Buf:
    __slots__ = ("name", "w", "r", "dsem", "psum", "nobar")

    def __init__(self, name, psum=False, nobar=False):
        self.nobar = nobar
        self.name = name
        self.w = []
        self.r = []
        self.dsem = None
        self.psum = psum


class Sched:
    def __init__(self, nc, stack):
        self.nc = nc
        self.stack = stack
        self.h = {"pe": nc.tensor, "act": nc.scalar, "dve": nc.vector, "pool": nc.gpsimd, "sp": nc.sync}
        self.sems = {}
        self.cnt = {}
        self.known = {e: {} for e in ENGS}
        self.nobar = set()
        self.epoch = 0
        self.nsem = 0

    def _sem(self, key):
        if key not in self.sems:
            self.sems[key] = self.stack.enter_context(self.nc.semaphore("s%d" % self.nsem))
            self.nsem += 1
            self.cnt[key] = 0
        return self.sems[key]

    def new_epoch(self):
        self.epoch += 1

    def _wait(self, eng, ev):
        key, val, src = ev
        if src == eng and eng == "pe":
            return
        if self.known[eng].get(key, 0) >= val:
            return
        self.known[eng][key] = val
        self.h[eng].wait_ge(self.sems[key], val)

    def deps(self, eng, r, w, isdma=False):
        need = {}

        def add(ev):
            key, val, src = ev
            if src == eng and eng == "pe":
                return
            if val > need.get(key, 0):
                need[key] = val
        for b in r:
            for ev in b.w:
                add(ev)
            if b.psum:
                for ev in b.r:
                    if ev[2] != eng:
                        add(ev)
        for b in w:
            for ev in b.w:
                add(ev)
            for ev in b.r:
                add(ev)
        for key, val in need.items():
            self._wait(eng, (key, val, None))

    def op(self, eng, fn, r=(), w=()):
        self.deps(eng, r, w)
        key = ("eng", eng, self.epoch)
        sem = self._sem(key)
        self.cnt[key] += 1
        fn(self.h[eng]).then_inc(sem, 1)
        ev = (key, self.cnt[key], eng)
        for b in r:
            b.r.append(ev)
        for b in w:
            b.w = [ev]
            b.r = []
        return ev

    def dma(self, eng, fn, r=(), w=(), wa=(), sembuf=None):
        self.deps(eng, r, tuple(w) + tuple(wa), isdma=True)
        sb = sembuf or (w[0] if w else r[0])
        if sb.dsem is None:
            sb.dsem = ("dma", sb.name, id(sb))
            if sb.nobar:
                self.nobar.add(sb.dsem)
        sem = self._sem(sb.dsem)
        self.cnt[sb.dsem] += 16
        fn(self.h[eng]).then_inc(sem, 16)
        ev = (sb.dsem, self.cnt[sb.dsem], "dma")
        for b in r:
            b.r.append(ev)
        for b in w:
            b.w = [ev]
            b.r = []
        for b in wa:
            b.w.append(ev)
        return ev

    def barrier(self, pe_waits=False):
        for eng in ENGS:
            if eng == "pe" and not pe_waits:
                continue
            for key in list(self.cnt):
                if key in self.nobar and not pe_waits:
                    continue
                if self.cnt[key] > 0:
                    self._wait(eng, (key, self.cnt[key], None))


def build_program(phases=("kv", "a", "b", "c", "out"), loads_rec=None):
    nc = bass.Bass("TRN2", target_bir_lowering=False)

    def din(name, shape):
        return nc.dram_tensor(name, list(shape), F32, kind="ExternalInput").ap()

    def dout(name, shape):
        return nc.dram_tensor(name, list(shape), F32, kind="ExternalOutput").ap()

    x_d = din("x", [NPASS, NT, D])
    mem_d = din("mem", [256, D])
    ck_d = din("ck", [16, 256, 1024])
    cv_d = din("cv", [16, 256, 1024])
    spin_d = din("sp_in", [16, 15, 1536])
    scin_d = din("sc_in", [16, 30, 1536])
    cs_d = din("consts", [128, C_END])
    id_d = din("ident", [128, 128])
    fg_d = din("fg", [128, D])
    w_in_d = din("w_in", [D, DIN])
    w_mk_d = din("w_mem_k", [D, 1024])
    w_mv_d = din("w_mem_v", [D, 1024])
    w_pool_d = din("w_pool", [4, 384, 384])
    w_pw_d = din("w_pw", [1536, 1536])
    w_out_d = din("w_out", [D, D])

    y_d = dout("y", [NPASS, W, D])
    mk_d = dout("mk", [256, 1024])
    mv_d = dout("mv", [256, 1024])
    psp_d = dout("ps_p", [NPASS, 15, 1536])
    csp_d = dout("cs_p", [NPASS, 30, 1536])
    pss_d = dout("ps_s", [16, 15, 1536])
    css_d = dout("cs_s", [16, 30, 1536])

    w_in_v = w_in_d.rearrange("(kc p) c -> p kc c", p=128)
    w_mk_v = w_mk_d.rearrange("(kc p) c -> p kc c", p=128)
    w_mv_v = w_mv_d.rearrange("(kc p) c -> p kc c", p=128)
    w_pw_v = w_pw_d.rearrange("(kc p) c -> p kc c", p=128)
    w_out_v = w_out_d.rearrange("(kc p) c -> p kc c", p=128)

    with contextlib.ExitStack() as st:
        S = Sched(nc, st)

        tcount = [0]

        def T(name, shape, dt, stack=st):
            tcount[0] += 1
            return stack.enter_context(nc.sbuf_tensor("%s_%d" % (name, tcount[0]), list(shape), dt))

        HT = T("HT", [128, 32, NT], BF16)
        CC = T("CC", [128, 32, W], BF16)
        slots = [T("slot%d" % i, [128, 8192], BF16) for i in range(NSLOT)]
        CS = T("CS", [128, C_END], F32)
        IDF = T("IDF", [128, 128], F32)
        IDB = T("IDB", [128, 128], BF16)
        ONES = T("ONES", [128, 128], BF16)
        HI = [T("HI%d" % i, [128, 544], BF16) for i in range(2)]
        LO = [T("LO%d" % i, [128, 544], BF16) for i in range(2)]
        HLb = [Buf("HL0"), Buf("HL1")]
        hl_n = [0]
        KT = T("KT", [128, 8, 256], BF16)
        VB = T("VB", [128, 2, 1024], BF16)
        HTb, CSb, IDb, KTb, VBb = Buf("HT"), Buf("CS"), Buf("ID"), Buf("KT"), Buf("VB")
        CCb = [Buf("CC%d" % i) for i in range(32)]
        slotb = [Buf("slot%d" % i, nobar=True) for i in range(NSLOT)]

        PZ = [st.enter_context(nc.psum_tensor("pz%d" % i, [128, 1024], F32)) for i in range(3)]
        PZb = [Buf("pz%d" % i, True) for i in range(3)]
        PQb = [[Buf("pz%dq%d" % (i, q), True) for q in range(4)] for i in range(3)]
        PBF = [st.enter_context(nc.psum_tensor("pb%d" % i, [128, 512], F32)) for i in range(2)]
        PB = [t_.bitcast(BF16) for t_ in PBF]
        PBb = [Buf("pb%d" % i, True) for i in range(2)]
        rot = {"pz": 0, "pb": 0, "pq": 0}

        def pz():
            i = rot["pz"] % 3
            rot["pz"] += 1
            return PZ[i], PZb[i]

        def pb():
            i = rot["pb"] % 2
            rot["pb"] += 1
            return PB[i], PBb[i]

        UMODE = ["all"]

        def pq():
            if UMODE[0] == "pz":
                cands = [0, 1, 2]
            elif UMODE[0] == "pb":
                cands = [3, 4]
            else:
                cands = [0, 1, 2, 3, 4]
            i = cands[rot["pq"] % len(cands)]
            rot["pq"] += 1
            if i < 3:
                return PZ[i][:, 0:256], PZb[i]
            return PBF[i - 3][:, 0:256], PBb[i - 3]

        WV = {"w_in": w_in_v, "w_mk": w_mk_v, "w_mv": w_mv_v, "w_pw": w_pw_v, "w_out": w_out_v}
        w_in_r = w_in_d[:, 0:9216].rearrange("(kc p) (r c) -> p kc r c", p=128, c=1536)
        recording = loads_rec is None
        rec = []
        loads = loads_rec or []
        wst = {"issue": 0, "get": 0}

        free_slots = list(range(NSLOT))
        load_slot = {}

        def w_fill(limit=None):
            if recording:
                return
            nfill = 0
            while free_slots and wst["issue"] < len(loads):
                if limit is not None and nfill >= limit:
                    return
                nfill += 1
                k = wst["issue"]
                wst["issue"] += 1
                name, c0, n, K = loads[k]
                s = free_slots.pop(0)
                load_slot[k] = s
                if name == "w_in_vg":
                    dv = slots[s][:, 0:K * n].rearrange("p (k r c) -> p k r c", k=K, r=2)
                    for r_, colb in enumerate((COL_V, COL_GG)):
                        src = w_in_v[:, :, colb + c0 * 128:colb + (c0 + 1) * 128]
                        dst = dv[:, :, r_, :]
                        if r_ == 0:
                            S.dma("pool", lambda e, dst=dst, src=src: e.dma_start(out=dst, in_=src), w=[slotb[s]])
                        else:
                            S.dma("pool", lambda e, dst=dst, src=src: e.dma_start(out=dst, in_=src), wa=[slotb[s]], sembuf=slotb[s])
                    continue
                src = WV[name][:, :, c0:c0 + n]
                dst = slots[s][:, 0:K * n].rearrange("p (k n) -> p k n", k=K)
                S.dma("pool", lambda e: e.dma_start(out=dst, in_=src), w=[slotb[s]])

        def w_get(name, c0, n=256, K=32):
            k = wst["get"]
            wst["get"] += 1
            if recording:
                rec.append((name, c0, n, K))
                s = k % NSLOT
            else:
                assert loads[k] == (name, c0, n, K), (k, loads[k], (name, c0, n, K))
                assert k in load_slot, "weight slot starvation at load %d" % k
                s = load_slot[k]
            return slots[s][:, 0:K * n].rearrange("p (k n) -> p k n", k=K), slotb[s], s

        def w_release(s):
            if recording:
                return
            free_slots.append(s)
            w_fill()

        S.dma("sp", lambda e: e.dma_start(out=CS[:], in_=cs_d), w=[CSb])
        S.dma("sp", lambda e: e.dma_start(out=IDF[:], in_=id_d), w=[IDb])
        S.op("dve", lambda e: e.tensor_copy(out=IDB[:], in_=IDF[:]), r=[IDb], w=[IDb])
        S.op("dve", lambda e: e.memset(ONES[:], 1.0), w=[IDb])
        w_fill(limit=1)

        def col(c):
            return CS[:, c:c + 1]

        def split(src, srcb, rows, n):
            k = hl_n[0] % 2
            hl_n[0] += 1
            S.op("act", lambda e: e.copy(out=HI[k][0:rows, 0:n], in_=src), r=[srcb], w=[HLb[k]])
            S.op("dve", lambda e: e.tensor_tensor(out=LO[k][0:rows, 0:n], in0=src, in1=HI[k][0:rows, 0:n], op=ALU.subtract),
                 r=[srcb, HLb[k]], w=[HLb[k]])
            return HI[k], LO[k], HLb[k]

        def tr32(out, outb, hi, lo, hlb, rows, c0, c1):
            S.op("pe", lambda e: e.matmul(out, lhsT=hi[0:rows, c0:c1], rhs=IDB[0:rows, 0:rows], start=True, stop=False),
                 r=[hlb, IDb], w=[outb])
            S.op("pe", lambda e: e.matmul(out, lhsT=lo[0:rows, c0:c1], rhs=IDB[0:rows, 0:rows], start=False, stop=True),
                 r=[hlb, IDb], w=[outb])

        def proj(view, sb, ci, K, c0, rhs_t, rhs_b, ncols):
            t, tb = pz()
            segs = []
            a = c0
            while a < ncols:
                b = min(a + 512, ncols)
                segs.append((a, b))
                a = b
            for kc in range(K):
                for (a, b) in segs:
                    S.op("pe", lambda e, kc=kc, a=a, b=b: e.matmul(
                        t[:, a - c0:b - c0], lhsT=view[:, kc, ci * 128:(ci + 1) * 128], rhs=rhs_t[:, kc, a:b],
                        start=(kc == 0), stop=(kc == K - 1)), r=[sb, rhs_b], w=[tb])
            return t, tb

        def norm_transpose(src_fn, tiles, gcol, dst, dstb, A, tagp, nbuf=2, npull=0):
            xt = [T("xt%d" % i, [128, D], F32, A) for i in range(nbuf)] * (2 // nbuf)
            xn = [T("xn%d" % i, [128, D], BF16, A) for i in range(nbuf)] * (2 // nbuf)
            st_ = T("nst", [128, 24], F32, A)
            xtb = [Buf("xt%d" % i) for i in range(nbuf)] * (2 // nbuf)
            xnb = [[Buf("xn%d_%d" % (i, hh)) for hh in range(2)] for i in range(nbuf)] * (2 // nbuf)
            stb = Buf("nst")
            for i, (r0, rows) in enumerate(tiles):
                b = i % 2
                S.dma("pool" if npull else "sp", lambda e, b=b, r0=r0, rows=rows: e.dma_start(out=xt[b][0:rows, :], in_=src_fn(r0, rows)),
                      w=[xtb[b]])
                S.op("act", lambda e, b=b, rows=rows, i=i: e.activation(
                    out=xn[b][0:rows, :], in_=xt[b][0:rows, :], func=AF.Square, accum_out=st_[0:rows, i:i + 1]),
                    r=[xtb[b]], w=[xnb[b][0], xnb[b][1], stb])
                S.op("act", lambda e, rows=rows, i=i: e.activation(
                    out=st_[0:rows, 8 + i:9 + i], in_=st_[0:rows, i:i + 1], func=AF.Sqrt, scale=1.0 / D, bias=EPS),
                    r=[stb], w=[stb])
                S.op("dve", lambda e, rows=rows, i=i: e.reciprocal(out=st_[0:rows, 16 + i:17 + i], in_=st_[0:rows, 8 + i:9 + i]),
                     r=[stb], w=[stb])
                S.op("act", lambda e, b=b, rows=rows, i=i: e.activation(
                    out=xn[b][0:rows, 0:2048], in_=xt[b][0:rows, 0:2048], func=AF.Copy, scale=st_[0:rows, 16 + i:17 + i]),
                    r=[xtb[b], stb], w=[xnb[b][0]])
                S.op("dve", lambda e, b=b, rows=rows, i=i: e.tensor_scalar(
                    out=xn[b][0:rows, 2048:4096], in0=xt[b][0:rows, 2048:4096], scalar1=st_[0:rows, 16 + i:17 + i], scalar2=None,
                    op0=ALU.mult), r=[xtb[b], stb], w=[xnb[b][1]])
                for q in range(4):
                    t, tb = pb()
                    tv = t[:, :].rearrange("p (k n) -> p k n", k=8)
                    for k in range(8):
                        kc = q * 8 + k
                        S.op("pe", lambda e, k=k, kc=kc, b=b, rows=rows: e.transpose(
                            out=tv[:, k, 0:rows], in_=xn[b][0:rows, kc * 128:(kc + 1) * 128],
                            identity=IDB[0:rows, 0:rows]), r=[xnb[b][q // 2], IDb], w=[tb])
                    S.op("dve", lambda e, q=q, r0=r0, rows=rows: e.tensor_tensor(
                        out=dst[:, q * 8:(q + 1) * 8, r0:r0 + rows], in0=tv[:, :, 0:rows],
                        in1=CS[:, gcol + q * 8:gcol + q * 8 + 8][:, :, None].broadcast_to([128, 8, rows]),
                        op=ALU.mult), r=[tb, CSb], w=[dstb])
                    if npull:
                        do_pull(1)

        XT_TILES = [(0, 128), (128, 128), (256, 128), (384, 128), (512, 64)]
        OT_TILES = [(0, 128), (128, 128), (256, 128), (384, 128), (512, 32)]


        XR = [T("XR%d" % i, [128, 256], F32) for i in range(2)]
        YT = [T("YT%d" % i, [128, 256], F32) for i in range(2)]
        JK = T("JK", [128, 256], BF16)
        SSQ = [T("SSQ%d" % i, [128, 5, 16], F32) for i in range(NPASS)]
        R2 = T("R2", [128, 16], F32)
        XRb = [Buf("XR%d" % i, nobar=True) for i in range(2)]
        YTb = [Buf("YT%d" % i, nobar=True) for i in range(2)]
        JKb, R2b = Buf("JK"), Buf("R2")
        SSQb = [Buf("SSQ%d" % i) for i in range(NPASS)]
        ydbs = [[Buf("yd%d_%d" % (p_, i)) for i in range(5)] for p_ in range(NPASS)]

        def out_main(p, YL=None, YLb=None):
            ydb = ydbs[p]
            units = [(ds, ti) for ds in range(16) for ti in range(5)]

            def xr_load(n):
                if n >= len(units):
                    return
                ds, ti = units[n]
                t0, rows = OT_TILES[ti]
                S.dma("sp", lambda e: e.dma_start(
                    out=XR[n % 2][0:rows, :], in_=x_d[p, 32 + t0:32 + t0 + rows, ds * 256:(ds + 1) * 256]), w=[XRb[n % 2]])
            xr_load(0)
            view = sb = None
            for n, (ds, ti) in enumerate(units):
                if ti == 0:
                    view, sb, sl = w_get("w_out", ds * 256)
                t0, rows = OT_TILES[ti]
                xr_load(n + 1)
                o, ob = pq()
                for kc in range(32):
                    S.op("pe", lambda e, kc=kc, o=o, view=view: e.matmul(
                        o[0:rows, :], lhsT=CC[:, kc, t0:t0 + rows], rhs=view[:, kc, :], start=(kc == 0), stop=(kc == 31)),
                        r=[sb, CCb[kc]], w=[ob])
                yb = n % 2
                if YL is not None:
                    ydst = YL[ti][0:rows, ds * 256:(ds + 1) * 256]
                    S.op("dve", lambda e, o=o, n=n, ydst=ydst: e.tensor_tensor(out=ydst, in0=o[0:rows, :],
                                                                               in1=XR[n % 2][0:rows, :], op=ALU.add),
                         r=[ob, XRb[n % 2], YLb[ti]], w=[YLb[ti]])
                    S.op("act", lambda e, ti=ti, ds=ds, ydst=ydst: e.activation(
                        out=JK[0:rows, :], in_=ydst, func=AF.Square, accum_out=SSQ[p][0:rows, ti, ds:ds + 1]),
                        r=[YLb[ti]], w=[JKb, SSQb[p]])
                    if ti == 4:
                        w_release(sl)
                    yield
                    continue
                S.op("dve", lambda e, o=o, yb=yb, n=n: e.tensor_tensor(out=YT[yb][0:rows, :], in0=o[0:rows, :],
                                                                       in1=XR[n % 2][0:rows, :], op=ALU.add),
                     r=[ob, XRb[n % 2]], w=[YTb[yb]])
                S.op("act", lambda e, yb=yb, ti=ti, ds=ds: e.activation(
                    out=JK[0:rows, :], in_=YT[yb][0:rows, :], func=AF.Square, accum_out=SSQ[p][0:rows, ti, ds:ds + 1]),
                    r=[YTb[yb]], w=[JKb, SSQb[p]])
                S.dma("act", lambda e, yb=yb, ds=ds: e.dma_start(
                    out=y_d[p, t0:t0 + rows, ds * 256:(ds + 1) * 256], in_=YT[yb][0:rows, :]),
                    r=[YTb[yb]], wa=[ydb[ti]], sembuf=YTb[yb])
                if ti == 4:
                    w_release(sl)
                yield

        def out_final(p, A, CBW=512):
            ydb = ydbs[p]
            YF = [T("YF%d" % i, [128, CBW], F32, A) for i in range(2)]
            FG = [T("FG%d" % i, [128, CBW], F32, A) for i in range(2)]
            YFb = [Buf("YF0"), Buf("YF1")]
            FGb = [Buf("FG0"), Buf("FG1")]
            for ti, (t0, rows) in enumerate(OT_TILES):
                S.op("dve", lambda e, ti=ti: e.tensor_reduce(out=R2[0:rows, ti:ti + 1], in_=SSQ[p][0:rows, ti, :], axis=AX.X, op=ALU.add),
                     r=[SSQb[p]], w=[R2b])
                S.op("act", lambda e, ti=ti: e.activation(out=R2[0:rows, 5 + ti:6 + ti], in_=R2[0:rows, ti:ti + 1], func=AF.Sqrt,
                                                          scale=1.0 / D, bias=EPS), r=[R2b], w=[R2b])
                S.op("dve", lambda e, ti=ti: e.reciprocal(out=R2[0:rows, 10 + ti:11 + ti], in_=R2[0:rows, 5 + ti:6 + ti]),
                     r=[R2b], w=[R2b])
            n = 0
            for cb in range(D // CBW):
                fb = cb % 2
                S.dma("sp", lambda e, fb=fb, cb=cb: e.dma_start(out=FG[fb][:], in_=fg_d[:, cb * CBW:(cb + 1) * CBW]), w=[FGb[fb]])
                for ti, (t0, rows) in enumerate(OT_TILES):
                    b = n % 2
                    n += 1
                    S.dma("sp", lambda e, b=b, cb=cb: e.dma_start(out=YF[b][0:rows, :], in_=y_d[p, t0:t0 + rows, cb * CBW:(cb + 1) * CBW]),
                          r=[ydb[ti]], w=[YFb[b]], sembuf=YFb[b])
                    S.op("dve", lambda e, b=b, ti=ti, fb=fb: e.scalar_tensor_tensor(
                        out=YF[b][0:rows, :], in0=YF[b][0:rows, :], scalar=R2[0:rows, 10 + ti:11 + ti], in1=FG[fb][0:rows, :],
                        op0=ALU.mult, op1=ALU.mult), r=[YFb[b], R2b, FGb[fb]], w=[YFb[b]])
                    S.dma("act", lambda e, b=b, cb=cb: e.dma_start(out=y_d[p, t0:t0 + rows, cb * CBW:(cb + 1) * CBW], in_=YF[b][0:rows, :]),
                          r=[YFb[b]], sembuf=YFb[b])
                    yield

        FGEN = [None]
        FPEND = [None]

        def do_fpull(n):
            for _ in range(n):
                if FGEN[0] is None:
                    return
                try:
                    next(FGEN[0])
                except StopIteration:
                    FGEN[0] = None
                    return


        def final_sb(p, YL, YLb, A):
            FG = [T("FGs%d" % i, [128, 512], F32, A) for i in range(2)]
            FGb = [Buf("FGs0"), Buf("FGs1")]
            for ti, (t0, rows) in enumerate(OT_TILES):
                S.op("dve", lambda e, ti=ti: e.tensor_reduce(out=R2[0:rows, ti:ti + 1], in_=SSQ[p][0:rows, ti, :], axis=AX.X, op=ALU.add),
                     r=[SSQb[p]], w=[R2b])
                S.op("act", lambda e, ti=ti: e.activation(out=R2[0:rows, 5 + ti:6 + ti], in_=R2[0:rows, ti:ti + 1], func=AF.Sqrt,
                                                          scale=1.0 / D, bias=EPS), r=[R2b], w=[R2b])
                S.op("dve", lambda e, ti=ti: e.reciprocal(out=R2[0:rows, 10 + ti:11 + ti], in_=R2[0:rows, 5 + ti:6 + ti]),
                     r=[R2b], w=[R2b])
            n = 0
            for ti, (t0, rows) in enumerate(OT_TILES):
                for cb in range(8):
                    fb = n % 2
                    n += 1
                    S.dma("sp", lambda e, fb=fb, cb=cb: e.dma_start(out=FG[fb][:], in_=fg_d[:, cb * 512:(cb + 1) * 512]), w=[FGb[fb]])
                    yv = YL[ti][0:rows, cb * 512:(cb + 1) * 512]
                    S.op("dve", lambda e, yv=yv, ti=ti, fb=fb: e.scalar_tensor_tensor(
                        out=yv, in0=yv, scalar=R2[0:rows, 10 + ti:11 + ti], in1=FG[fb][0:rows, :],
                        op0=ALU.mult, op1=ALU.mult), r=[YLb[ti], R2b, FGb[fb]], w=[YLb[ti]])
                S.dma("act" if ti % 2 else "sp", lambda e, ti=ti: e.dma_start(out=y_d[p, t0:t0 + rows, :], in_=YL[ti][0:rows, :]),
                      r=[YLb[ti]])

        GEN = [None]

        def do_pull(n):
            for _ in range(n):
                if GEN[0] is None:
                    return
                try:
                    next(GEN[0])
                except StopIteration:
                    GEN[0] = None
                    return

        HM = CC[:, :, :].rearrange("p a b -> p (a b)")[:, 0:8192].rearrange("p (k n) -> p k n", k=32)
        HMb = Buf("HM")

        def kv_proj(A):
            KST = T("KST", [128, 2, 128], F32, A)
            KSTb = Buf("KST")
            for which_ in range(2):
                for i in range(4):
                    view, sb, sl = w_get("w_mk" if which_ == 0 else "w_mv", i * 256)
                    for ci in range(2):
                        c = 2 * i + ci
                        t, tb = pq()
                        for kc in range(32):
                            S.op("pe", lambda e, kc=kc, t=t, view=view, ci=ci: e.matmul(
                                t[:, 0:256], lhsT=view[:, kc, ci * 128:(ci + 1) * 128], rhs=HM[:, kc, :],
                                start=(kc == 0), stop=(kc == 31)), r=[sb, HMb] + CCb[12:16], w=[tb])
                        if which_ == 0:
                            S.op("act", lambda e, c=c, t=t: e.copy(out=KT[:, c, :], in_=t[:, 0:256]), r=[tb], w=[KTb])
                        hi, lo, hlb = split(t[:, 0:256], tb, 128, 256)
                        t2, t2b = pq()
                        for mc in range(2):
                            tr32(t2[:, mc * 128:(mc + 1) * 128], t2b, hi, lo, hlb, 128, mc * 128, (mc + 1) * 128)
                        S.op("act", lambda e, t2=t2: e.copy(out=KST[:], in_=t2[:, 0:256].rearrange("p (m n) -> p m n", m=2)),
                             r=[t2b], w=[KSTb])
                        if which_ == 1:
                            S.op("act", lambda e, c=c: e.copy(out=VB[:, :, c * 128:(c + 1) * 128], in_=KST[:]), r=[KSTb], w=[VBb])
                        od = mk_d if which_ == 0 else mv_d
                        S.dma("act", lambda e, c=c, od=od: e.dma_start(
                            out=od.rearrange("(mc q) c -> q mc c", q=128)[:, :, c * 128:(c + 1) * 128], in_=KST[:]),
                            r=[KSTb], sembuf=KSTb)
                        yield
                    w_release(sl)

        def run(which, p):
            if which == "p0":
                with contextlib.ExitStack() as A:
                    norm_transpose(lambda r0, rows, p=p: x_d[p, r0:r0 + rows, :], XT_TILES, C_G, HT, HTb, A, "x", npull=0)
                    S.barrier()

            if which == "kv" and "kv" in phases:
                with contextlib.ExitStack() as A2:
                    norm_transpose(lambda r0, rows: mem_d[r0:r0 + rows, :], [(0, 128), (128, 128)], C_MG, HM, HMb, A2, "m", nbuf=1)
                    S.barrier()
                S.deps("pool", [HMb], [])
                w_fill()

            if which == "a" and "a" in phases:
                with contextlib.ExitStack() as A:
                    PL = T("PL", [128, 12, W], BF16, A)
                    if FPEND[0] is not None:
                        FGEN[0] = out_final(FPEND[0], A)
                        FPEND[0] = None
                    WP = T("WP", [128, 4, 3, 384], BF16, A)
                    WPb = Buf("WP")
                    for g in range(4):
                        S.dma("pool", lambda e, g=g: e.dma_start(
                            out=WP[:, g], in_=w_pool_d[g].rearrange("(cc p) d -> p cc d", p=128)), wa=[WPb], sembuf=WPb)
                    EU = [T("EU%d" % i, [128, W], F32, A) for i in range(2)]
                    ES = [T("ES%d" % i, [128, 8, 19], F32, A) for i in range(2)]
                    TW = [T("TW%d" % i, [128, W], F32, A) for i in range(2)]
                    TS = [T("TS%d" % i, [128, 8, 19], F32, A) for i in range(2)]
                    t16 = T("t16", [128, 16], F32, A)
                    SGt = [T("SGt%d" % i, [128, W], F32, A) for i in range(2)]
                    UT = T("UT", [128, 12, 15], F32, A)
                    USA = T("USA", [128, 12, 32], F32, A)
                    STJ = [T("STJ%d" % i, [128, 128], F32, A) for i in range(2)]
                    STG = [T("STG%d" % i, [32, 512], F32, A) for i in range(2)]
                    PLb = [Buf("PL%d" % j) for j in range(12)]
                    EUb, ESb = [Buf("EU0"), Buf("EU1")], [Buf("ES0"), Buf("ES1")]
                    TWb, TSb = [Buf("TW0"), Buf("TW1")], [Buf("TS0"), Buf("TS1")]
                    t16b, UTb, USAb = Buf("t16"), Buf("UT"), Buf("USA")
                    SGtb = [Buf("SGt0"), Buf("SGt1")]
                    STJb = [Buf("STJ0"), Buf("STJ1")]
                    STGb = [Buf("STG0"), Buf("STG1")]
                    pssb = Buf("pss")
                    spv = spin_d[8 * p:8 * p + 8].rearrange("s r c -> (s r) c")
                    S.dma("sp", lambda e: e.dma_start(
                        out=pss_d[8 * p:8 * p + 8, 0:11, :].rearrange("s r c -> s (r c)"),
                        in_=spin_d[8 * p:8 * p + 8, 4:15, :].rearrange("s r c -> s (r c)")), wa=[pssb], sembuf=pssb)
                    for i in range(6):
                        view, sb, sl = w_get("w_in", COL_U + i * 256)
                        for ci in range(2):
                            j = 2 * i + ci
                            b = j % 2
                            g = j // 3
                            wdt = POOLW[g]
                            S.dma("sp", lambda e, b=b, j=j: e.dma_start(out=STJ[b][0:120, :], in_=spv[:, j * 128:(j + 1) * 128]),
                                  w=[STJb[b]])
                            hi, lo, hlb = split(STJ[b][0:120, :], STJb[b], 120, 128)
                            t, tb = proj(view, sb, ci, 32, 0, HT, HTb, NT)
                            S.op("act", lambda e, b=b, t=t: e.copy(out=EU[b][:], in_=t[:, 0:W]), r=[tb], w=[EUb[b]])
                            S.op("act", lambda e, j=j, t=t: e.copy(out=USA[:, j, :], in_=t[:, W:NT]), r=[tb], w=[USAb])
                            t2, t2b = pz()
                            tr32(t2[:, 0:120], t2b, hi, lo, hlb, 120, 0, 128)
                            S.op("dve", lambda e, b=b, t2=t2: e.tensor_copy(
                                out=ES[b][:, :, 0:15], in_=t2[:, 0:120].rearrange("p (s r) -> p s r", s=8)), r=[t2b], w=[ESb[b]])
                            S.op("dve", lambda e, b=b, j=j: e.tensor_copy(
                                out=ES[b][:, :, 15:19], in_=USA[:, j, :].rearrange("p (s r) -> p s r", s=8)), r=[USAb], w=[ESb[b]])
                            S.op("dve", lambda e, b=b, j=j: e.tensor_copy(out=UT[:, j, :], in_=EU[b][:, W - 15:W]), r=[EUb[b]], w=[UTb])
                            src, srcb, ssrc, ssrcb = EU[b], EUb[b], ES[b], ESb[b]
                            for l in range(1, g + 2):
                                off = 1 << (l - 1)
                                lo = (1 << l) - 1
                                d_, db_ = TW[l % 2], TWb[l % 2]
                                S.op("dve", lambda e, d_=d_, src=src, lo=lo, off=off: e.tensor_tensor(
                                    out=d_[:, lo:W], in0=src[:, lo:W], in1=src[:, lo - off:W - off], op=ALU.add),
                                    r=[srcb], w=[db_])
                                ds_, dsb_ = TS[l % 2], TSb[l % 2]
                                S.op("dve", lambda e, ds_=ds_, ssrc=ssrc, lo=lo, off=off: e.tensor_tensor(
                                    out=ds_[:, :, lo:19], in0=ssrc[:, :, lo:19], in1=ssrc[:, :, lo - off:19 - off], op=ALU.add),
                                    r=[ssrcb], w=[dsb_])
                                src, srcb, ssrc, ssrcb = d_, db_, ds_, dsb_
                            S.op("dve", lambda e, j=j, src=src, b=b, wdt=wdt: e.scalar_tensor_tensor(
                                out=PL[:, j, 16:512], in0=src[:, 48:W], scalar=1.0 / wdt, in1=EU[b][:, 48:W],
                                op0=ALU.mult, op1=ALU.subtract), r=[srcb, EUb[b]], w=[PLb[j]])
                            ic = C_INV + (p * 4 + g) * 16
                            S.op("dve", lambda e, src=src, ic=ic: e.tensor_tensor(
                                out=t16[:], in0=src[:, 32:48], in1=CS[:, ic:ic + 16], op=ALU.mult), r=[srcb, CSb], w=[t16b])
                            S.op("dve", lambda e, j=j, b=b: e.tensor_tensor(
                                out=PL[:, j, 0:16], in0=t16[:], in1=EU[b][:, 32:48], op=ALU.subtract),
                                r=[t16b, EUb[b]], w=[PLb[j]])
                            S.op("dve", lambda e, j=j, ssrc=ssrc, b=b, wdt=wdt: e.scalar_tensor_tensor(
                                out=PL[:, j, 512:W].rearrange("p (s r) -> p s r", s=8), in0=ssrc[:, :, 15:19],
                                scalar=1.0 / wdt, in1=ES[b][:, :, 15:19], op0=ALU.mult, op1=ALU.subtract),
                                r=[ssrcb, ESb[b]], w=[PLb[j]])
                            do_fpull(2)
                        w_release(sl)
                    for q in range(3):
                        t, tb = pz()
                        hi, lo, hlb = split(UT[:, 4 * q:4 * q + 4, :].rearrange("p a b -> p (a b)"), UTb, 128, 60)
                        for k in range(4):
                            tr32(t[0:15, k * 128:(k + 1) * 128], tb, hi, lo, hlb, 128, k * 15, (k + 1) * 15)
                        hi, lo, hlb = split(USA[:, 4 * q:4 * q + 4, :].rearrange("p a b -> p (a b)"), USAb, 128, 128)
                        for k in range(4):
                            tr32(t[0:32, 512 + k * 128:512 + (k + 1) * 128], tb, hi, lo, hlb, 128, k * 32, (k + 1) * 32)
                        S.op("act", lambda e, t=t: e.copy(out=STG[0][0:15, :], in_=t[0:15, 0:512]), r=[tb], w=[STGb[0]])
                        S.op("act", lambda e, t=t: e.copy(out=STG[1][0:32, :], in_=t[0:32, 512:1024]), r=[tb], w=[STGb[1]])
                        S.dma("act", lambda e, q=q: e.dma_start(out=psp_d[p, :, q * 512:(q + 1) * 512], in_=STG[0][0:15, :]),
                              r=[STGb[0]], sembuf=STGb[0])
                        for s in range(8):
                            S.dma("act", lambda e, q=q, s=s: e.dma_start(
                                out=pss_d[8 * p + s, 11:15, q * 512:(q + 1) * 512], in_=STG[1][4 * s:4 * s + 4, :]),
                                r=[STGb[1]], sembuf=STGb[1])
                    for i in range(6):
                        view, sb, sl = w_get("w_in", COL_GA + i * 256)
                        for ci in range(2):
                            j = 2 * i + ci
                            b = j % 2
                            g, dj = j // 3, j % 3
                            t, tb = proj(view, sb, ci, 32, 32, HT, HTb, NT)
                            S.op("act", lambda e, b=b, t=t: e.activation(out=SGt[b][:], in_=t[:, 0:W], func=AF.Silu),
                                 r=[tb], w=[SGtb[b]])
                            m, mb = pz()
                            for cc in range(3):
                                for (a, bb) in ((0, 512), (512, W)):
                                    S.op("pe", lambda e, cc=cc, a=a, bb=bb, g=g, dj=dj, m=m: e.matmul(
                                        m[:, a:bb], lhsT=WP[:, g, cc, dj * 128:(dj + 1) * 128], rhs=PL[:, 3 * g + cc, a:bb],
                                        start=(cc == 0), stop=(cc == 2)), r=[WPb, PLb[3 * g + cc]], w=[mb])
                            S.op("dve", lambda e, j=j, b=b, m=m: e.scalar_tensor_tensor(
                                out=CC[:, j, :], in0=m[:, 0:W], scalar=col(C_PSC + j), in1=SGt[b][:],
                                op0=ALU.mult, op1=ALU.mult), r=[mb, SGtb[b], CSb], w=[CCb[j]])
                            do_fpull(2)
                        w_release(sl)
                    do_fpull(1000)
                    S.barrier()

            if which == "b" and "b" in phases:
                with contextlib.ExitStack() as A:
                    YB = T("YB", [128, 12, W], F32, A)
                    if p == 0 and "kv" in phases:
                        GEN[0] = kv_proj(A)
                    EA = [T("EA%d" % i, [128, W], F32, A) for i in range(2)]
                    ESA = [T("ESA%d" % i, [128, 8, 34], F32, A) for i in range(2)]
                    SG = [T("SG%d" % i, [128, NT], F32, A) for i in range(2)]
                    SQ = T("SQ", [128, W], F32, A)
                    SS = T("SSm", [128, W], F32, A)
                    QS = T("QSm", [128, W], F32, A)
                    MU = T("MU", [128, W], F32, A)
                    RS = T("RS", [128, W], F32, A)
                    Y1 = MU[:, 0:512]
                    AT = T("AT", [128, 12, 30], F32, A)
                    ASA = T("ASA", [128, 12, 32], F32, A)
                    STJ = [T("STJc%d" % i, [128, 2, 128], F32, A) for i in range(2)]
                    STG = [SG[i][0:32, 0:512] for i in range(2)]

                    def lnloc(kc):
                        if kc < 8:
                            return CC[:, 24 + kc, :], CCb[24 + kc]
                        q_ = kc - 8
                        return EA[q_ // 2].bitcast(BF16)[:, (q_ % 2) * W:(q_ % 2 + 1) * W], LNXb[q_]
                    YBb = [Buf("YB%d" % j) for j in range(12)]
                    EAb, ESAb = [Buf("EA0"), Buf("EA1")], [Buf("ESA0"), Buf("ESA1")]
                    LNXb = [EAb[0], EAb[0], EAb[1], EAb[1]]
                    SGb = [Buf("SG0"), Buf("SG1")]
                    YS1b, SQb, SSb, QSb, MUb, RSb = Buf("YS1"), Buf("SQ"), Buf("SS"), Buf("QS"), Buf("MU"), Buf("RS")
                    Y1b = MUb
                    ATb, ASAb = Buf("AT"), Buf("ASA")
                    STJb = [Buf("STJc0"), Buf("STJc1")]
                    STGb = SGb
                    cssb = Buf("css")
                    scv = scin_d[8 * p:8 * p + 8].rearrange("s r c -> (s r) c")
                    S.dma("sp", lambda e: e.dma_start(
                        out=css_d[8 * p:8 * p + 8, 0:26, :].rearrange("s r c -> s (r c)"),
                        in_=scin_d[8 * p:8 * p + 8, 4:30, :].rearrange("s r c -> s (r c)")), wa=[cssb], sembuf=cssb)
                    for i in range(6):
                        for ci in range(2):
                            j = 2 * i + ci
                            b = j % 2
                            vgv, vgsb, vgsl = w_get("w_in_vg", j)
                            for mc in range(2):
                                S.dma("sp", lambda e, b=b, j=j, mc=mc: e.dma_start(
                                    out=STJ[b][0:120, mc, :], in_=scv[mc * 120:(mc + 1) * 120, j * 128:(j + 1) * 128]),
                                    wa=[STJb[b]] if mc else (), w=() if mc else [STJb[b]], sembuf=STJb[b])
                            tv, tvb = proj(vgv, vgsb, 0, 32, 0, HT, HTb, NT)
                            tg, tgb = proj(vgv, vgsb, 1, 32, 0, HT, HTb, NT)
                            w_release(vgsl)
                            S.op("act", lambda e, b=b, tg=tg: e.activation(out=SG[b][:], in_=tg[:, 0:NT], func=AF.Sigmoid),
                                 r=[tgb], w=[SGb[b]])
                            S.op("dve", lambda e, b=b, tv=tv: e.tensor_tensor(out=EA[b][:], in0=tv[:, 0:W], in1=SG[b][:, 0:W],
                                                                              op=ALU.mult), r=[tvb, SGb[b]], w=[EAb[b]])
                            S.op("dve", lambda e, b=b, j=j, tv=tv: e.tensor_tensor(out=ASA[:, j, :], in0=tv[:, W:NT], in1=SG[b][:, W:NT],
                                                                                   op=ALU.mult), r=[tvb, SGb[b]], w=[ASAb])
                            t2, t2b = pz()
                            hi, lo, hlb = split(STJ[b][0:120, :, :].rearrange("p a b -> p (a b)"), STJb[b], 120, 256)
                            for mc in range(2):
                                tr32(t2[:, mc * 120:(mc + 1) * 120], t2b, hi, lo, hlb, 120, mc * 128, (mc + 1) * 128)
                            S.op("act", lambda e, b=b, t2=t2: e.copy(
                                out=ESA[b][:, :, 0:30], in_=t2[:, 0:240].rearrange("p (s r) -> p s r", s=8)), r=[t2b], w=[ESAb[b]])
                            S.op("dve", lambda e, b=b, j=j: e.tensor_copy(
                                out=ESA[b][:, :, 30:34], in_=ASA[:, j, :].rearrange("p (s r) -> p s r", s=8)), r=[ASAb], w=[ESAb[b]])
                            S.op("dve", lambda e, b=b, j=j: e.tensor_copy(out=AT[:, j, :], in_=EA[b][:, W - 30:W]), r=[EAb[b]], w=[ATb])
                            wc = C_WDW + j * 31
                            ysv = YB[:, j, 512:W].rearrange("p (s r) -> p s r", s=8)
                            S.op("dve", lambda e, b=b, j=j, wc=wc: e.tensor_scalar(
                                out=YB[:, j, 0:512], in0=EA[b][:, 2:514], scalar1=col(wc), scalar2=col(C_BDW + j),
                                op0=ALU.mult, op1=ALU.add), r=[EAb[b], CSb], w=[YBb[j]])
                            S.op("dve", lambda e, b=b, wc=wc: e.tensor_scalar(
                                out=Y1[:], in0=EA[b][:, 3:515], scalar1=col(wc + 1), scalar2=None, op0=ALU.mult),
                                r=[EAb[b], CSb], w=[Y1b])
                            for k in range(2, 31):
                                if k % 5 == 0 and (p > 0 or k == 5):
                                    do_pull(1)
                                if k % 2 == 0:
                                    S.op("dve", lambda e, b=b, j=j, k=k, wc=wc: e.scalar_tensor_tensor(
                                        out=YB[:, j, 0:512], in0=EA[b][:, 2 + k:514 + k], scalar=col(wc + k), in1=YB[:, j, 0:512],
                                        op0=ALU.mult, op1=ALU.add), r=[EAb[b], CSb, YBb[j]], w=[YBb[j]])
                                else:
                                    S.op("dve", lambda e, b=b, k=k, wc=wc: e.scalar_tensor_tensor(
                                        out=Y1[:], in0=EA[b][:, 2 + k:514 + k], scalar=col(wc + k), in1=Y1[:],
                                        op0=ALU.mult, op1=ALU.add), r=[EAb[b], CSb, Y1b], w=[Y1b])
                            S.op("dve", lambda e, j=j: e.tensor_tensor(out=YB[:, j, 0:512], in0=YB[:, j, 0:512], in1=Y1[:], op=ALU.add),
                                 r=[YBb[j], Y1b], w=[YBb[j]])
                            tmpv = SQ[:, 0:248].rearrange("p (s k) -> p s k", s=8)
                            for t_ in range(4):
                                S.op("dve", lambda e, b=b, wc=wc, t_=t_, tmpv=tmpv: e.tensor_tensor(
                                    out=tmpv, in0=ESA[b][:, :, t_:t_ + 31],
                                    in1=CS[:, wc:wc + 31][:, None, :].broadcast_to([128, 8, 31]), op=ALU.mult),
                                    r=[ESAb[b], CSb], w=[SQb])
                                S.op("dve", lambda e, ysv=ysv, t_=t_, tmpv=tmpv: e.tensor_reduce(
                                    out=ysv[:, :, t_:t_ + 1], in_=tmpv, axis=AX.X, op=ALU.add), r=[SQb], w=[YBb[j]])
                            S.op("dve", lambda e, ysv=ysv, j=j: e.tensor_scalar(out=ysv, in0=ysv, scalar1=col(C_BDW + j), scalar2=None,
                                                                              op0=ALU.add), r=[YBb[j], CSb], w=[YBb[j]])
                            S.op("act", lambda e, j=j: e.activation(out=SQ[:], in_=YB[:, j, :], func=AF.Square), r=[YBb[j]], w=[SQb])
                            if j == 0:
                                S.op("dve", lambda e, j=j: e.tensor_copy(out=SS[:], in_=YB[:, j, :]), r=[YBb[j]], w=[SSb])
                                S.op("dve", lambda e: e.tensor_copy(out=QS[:], in_=SQ[:]), r=[SQb], w=[QSb])
                            else:
                                S.op("dve", lambda e, j=j: e.tensor_tensor(out=SS[:], in0=SS[:], in1=YB[:, j, :], op=ALU.add),
                                     r=[YBb[j], SSb], w=[SSb])
                                S.op("dve", lambda e: e.tensor_tensor(out=QS[:], in0=QS[:], in1=SQ[:], op=ALU.add),
                                     r=[SQb, QSb], w=[QSb])
                    do_pull(8)
                    for q in range(3):
                        t, tb = pz()
                        hi, lo, hlb = split(AT[:, 4 * q:4 * q + 4, :].rearrange("p a b -> p (a b)"), ATb, 128, 120)
                        for k in range(4):
                            tr32(t[0:30, k * 128:(k + 1) * 128], tb, hi, lo, hlb, 128, k * 30, (k + 1) * 30)
                        hi, lo, hlb = split(ASA[:, 4 * q:4 * q + 4, :].rearrange("p a b -> p (a b)"), ASAb, 128, 128)
                        for k in range(4):
                            tr32(t[0:32, 512 + k * 128:512 + (k + 1) * 128], tb, hi, lo, hlb, 128, k * 32, (k + 1) * 32)
                        S.op("act", lambda e, t=t: e.copy(out=STG[0][0:30, :], in_=t[0:30, 0:512]), r=[tb], w=[STGb[0]])
                        S.op("act", lambda e, t=t: e.copy(out=STG[1][0:32, :], in_=t[0:32, 512:1024]), r=[tb], w=[STGb[1]])
                        S.dma("act", lambda e, q=q: e.dma_start(out=csp_d[p, :, q * 512:(q + 1) * 512], in_=STG[0][0:30, :]),
                              r=[STGb[0]], sembuf=STGb[0])
                        for s in range(8):
                            S.dma("act", lambda e, q=q, s=s: e.dma_start(
                                out=css_d[8 * p + s, 26:30, q * 512:(q + 1) * 512], in_=STG[1][4 * s:4 * s + 4, :]),
                                r=[STGb[1]], sembuf=STGb[1])
                    ta, tab = pz()
                    tq, tqb = pz()
                    for (src_, srcb_, dst_, dstb_) in ((SS, SSb, ta, tab), (QS, QSb, tq, tqb)):
                        hi, lo, hlb = split(src_[:, :], srcb_, 128, W)
                        for (a, bb) in ((0, 512), (512, W)):
                            S.op("pe", lambda e, a=a, bb=bb, dst_=dst_, hi=hi: e.matmul(dst_[:, a:bb], lhsT=ONES[:], rhs=hi[:, a:bb],
                                                                                      start=True, stop=False), r=[IDb, hlb], w=[dstb_])
                            S.op("pe", lambda e, a=a, bb=bb, dst_=dst_, lo=lo: e.matmul(dst_[:, a:bb], lhsT=ONES[:], rhs=lo[:, a:bb],
                                                                                      start=False, stop=True), r=[IDb, hlb], w=[dstb_])
                    S.op("dve", lambda e: e.tensor_scalar(out=MU[:], in0=ta[:, 0:W], scalar1=1.0 / 1536, scalar2=None, op0=ALU.mult),
                         r=[tab], w=[MUb])
                    S.op("dve", lambda e: e.tensor_tensor(out=SQ[:], in0=MU[:], in1=MU[:], op=ALU.mult), r=[MUb], w=[SQb])
                    S.op("dve", lambda e: e.scalar_tensor_tensor(out=RS[:], in0=tq[:, 0:W], scalar=1.0 / 1536, in1=SQ[:],
                                                                  op0=ALU.mult, op1=ALU.subtract), r=[tqb, SQb], w=[RSb])
                    S.op("act", lambda e: e.activation(out=RS[:], in_=RS[:], func=AF.Sqrt, scale=1.0, bias=EPS), r=[RSb], w=[RSb])
                    S.op("dve", lambda e: e.reciprocal(out=RS[:], in_=RS[:]), r=[RSb], w=[RSb])
                    TT, TTb = [SS, QS], [SSb, QSb]
                    for j in range(12):
                        b = j % 2
                        S.op("dve", lambda e, j=j, b=b: e.tensor_tensor(out=TT[b][:], in0=YB[:, j, :], in1=MU[:], op=ALU.subtract),
                             r=[YBb[j], MUb], w=[TTb[b]])
                        S.op("dve", lambda e, b=b: e.tensor_tensor(out=TT[b][:], in0=TT[b][:], in1=RS[:], op=ALU.mult),
                             r=[TTb[b], RSb], w=[TTb[b]])
                        lo_, lob_ = lnloc(j)
                        S.op("act", lambda e, j=j, b=b, lo_=lo_: e.activation(out=lo_, in_=TT[b][:], func=AF.Silu,
                                                                     scale=col(C_LNG + j), bias=col(C_LNB + j)),
                             r=[TTb[b], CSb], w=[lob_])
                    for i in range(6):
                        pv, psb, psl = w_get("w_pw", i * 256, 256, 12)
                        gv, gsb, gsl = w_get("w_in", COL_GB + i * 256)
                        for ci in range(2):
                            dj = 2 * i + ci
                            b = dj % 2
                            tg, tgb = proj(gv, gsb, ci, 32, 32, HT, HTb, NT)
                            S.op("act", lambda e, b=b, tg=tg: e.activation(out=SG[b][:, 0:W], in_=tg[:, 0:W], func=AF.Silu),
                                 r=[tgb], w=[SGb[b]])
                        for ci in range(2):
                            dj = 2 * i + ci
                            b = dj % 2
                            o, ob = pz()
                            for kc in range(12):
                                for (a, bb) in ((0, 512), (512, W)):
                                    lo_, lob_ = lnloc(kc)
                                    S.op("pe", lambda e, kc=kc, a=a, bb=bb, ci=ci, o=o, pv=pv, lo_=lo_: e.matmul(
                                        o[:, a:bb], lhsT=pv[:, kc, ci * 128:(ci + 1) * 128], rhs=lo_[:, a:bb],
                                        start=(kc == 0), stop=(kc == 11)), r=[psb, lob_], w=[ob])
                            S.op("dve", lambda e, b=b, dj=dj, o=o: e.tensor_tensor(
                                out=CC[:, 12 + dj, :], in0=o[:, 0:W], in1=SG[b][:, 0:W], op=ALU.mult),
                                r=[ob, SGb[b]], w=[CCb[12 + dj]])
                        w_release(psl)
                        w_release(gsl)
                    S.barrier()

            if which == "c" and "c" in phases:
                with contextlib.ExitStack() as A:
                    QT = T("QT", [128, 8, W], BF16, A)
                    PT = T("PT", [128, 8, W], BF16, A)
                    QM = T("QM", [128, 8, 8, 32], BF16, A)
                    Pf = [T("Pf0", [128, 4, 256], F32, A)] * 2
                    Pb = [T("Pb0", [128, 4, 256], BF16, A)] * 2
                    SM = [T("SM%d" % i, [128, 16], F32, A) for i in range(2)]
                    KB = [T("KB%d" % i, [128, 2, 1024], BF16, A) for i in range(2)]
                    VS = [T("VS%d" % i, [128, 2, 1024], BF16, A) for i in range(2)]
                    KTs = [T("KTs0", [128, 8, 256], BF16, A)] * 2
                    SGC = [T("SGC%d" % i, [128, W], F32, A) for i in range(2)]
                    OSs = T("OSs", [128, 8, 32], F32, A)
                    QTb, PTb, QMb, OSsb = Buf("QT"), Buf("PT"), Buf("QM"), Buf("OSs")
                    Pfb, Pbb, SMb = [Buf("Pf0")] * 2, [Buf("Pb0")] * 2, [Buf("SM0"), Buf("SM1")]
                    KBb, VSb, KTsb = [Buf("KB0"), Buf("KB1")], [Buf("VS0"), Buf("VS1")], [Buf("KTs0")] * 2
                    SGCb = [Buf("SGC0"), Buf("SGC1")]
                    for i in range(4):
                        view, sb, sl = w_get("w_in", COL_Q + i * 256)
                        for ci in range(2):
                            c = 2 * i + ci
                            t, tb = proj(view, sb, ci, 32, 32, HT, HTb, NT)
                            S.op("act", lambda e, c=c, t=t: e.copy(out=QT[:, c, :], in_=t[:, 0:W]), r=[tb], w=[QTb])
                        w_release(sl)
                    S.op("dve", lambda e: e.memset(QM[:], 0.0), w=[QMb])
                    for s in range(8):
                        S.op("dve", lambda e, s=s: e.tensor_copy(out=QM[:, s, :, 4 * s:4 * s + 4],
                                                                 in_=QT[:, :, 512 + 4 * s:516 + 4 * s]), r=[QTb], w=[QMb])
                    sm_n = [0]

                    def softmax(heads, rows):
                        b = sm_n[0] % 2
                        sm_n[0] += 1
                        sm = SM[b]
                        for h, (hap, hb) in enumerate(heads):
                            S.op("dve", lambda e, h=h, hap=hap: e.tensor_reduce(
                                out=sm[0:rows, h:h + 1], in_=hap, axis=AX.X, op=ALU.max), r=[hb], w=[SMb[b]])
                        S.op("dve", lambda e: e.tensor_scalar(out=sm[0:rows, 4:8], in0=sm[0:rows, 0:4], scalar1=-1.0 / 16, scalar2=None,
                                                               op0=ALU.mult), r=[SMb[b]], w=[SMb[b]])
                        for h, (hap, hb) in enumerate(heads):
                            S.op("act", lambda e, h=h, hap=hap: e.activation(
                                out=Pf[b][0:rows, h, :], in_=hap, func=AF.Exp, scale=1.0 / 16,
                                bias=sm[0:rows, 4 + h:5 + h], accum_out=sm[0:rows, 8 + h:9 + h]),
                                r=[hb, SMb[b]], w=[Pfb[b], SMb[b]])
                        S.op("dve", lambda e: e.reciprocal(out=sm[0:rows, 12:16], in_=sm[0:rows, 8:12]), r=[SMb[b]], w=[SMb[b]])
                        S.op("dve", lambda e: e.tensor_tensor(
                            out=Pb[b][0:rows], in0=Pf[b][0:rows], in1=sm[0:rows, 12:16][:, :, None].broadcast_to([rows, 4, 256]),
                            op=ALU.mult), r=[Pfb[b], SMb[b]], w=[Pbb[b]])
                        return b

                    def p_transpose(b, rows, c0):
                        t, tb = pb()
                        tv = t[:, :].rearrange("p (k n) -> p k n", k=8)
                        for h in range(4):
                            for mc in range(2):
                                S.op("pe", lambda e, h=h, mc=mc: e.transpose(
                                    out=tv[:, 2 * h + mc, 0:rows], in_=Pb[b][0:rows, h, mc * 128:(mc + 1) * 128],
                                    identity=IDB[0:rows, 0:rows]), r=[Pbb[b], IDb], w=[tb])
                        S.op("act", lambda e: e.copy(out=PT[:, :, c0:c0 + rows], in_=tv[:, :, 0:rows]), r=[tb], w=[PTb])

                    for ti in range(4):
                        t, tb = pz()
                        for h in range(4):
                            for ec in range(2):
                                S.op("pe", lambda e, h=h, ec=ec, ti=ti, t=t: e.matmul(
                                    t[:, h * 256:(h + 1) * 256], lhsT=QT[:, 2 * h + ec, ti * 128:(ti + 1) * 128],
                                    rhs=KT[:, 2 * h + ec, :], start=(ec == 0), stop=(ec == 1)), r=[QTb, KTb], w=[tb])
                        b = softmax([(t[:, h * 256:(h + 1) * 256], tb) for h in range(4)], 128)
                        p_transpose(b, 128, ti * 128)
                    tS, tSb = pz()
                    tS2, tS2b = pz()
                    shead = [((tS if h < 2 else tS2)[0:32, (h % 2) * 512:(h % 2) * 512 + 256], (tSb if h < 2 else tS2b)) for h in range(4)]
                    for s in range(8):
                        b = s % 2
                        sg = 8 * p + s
                        S.dma("pool", lambda e, b=b, sg=sg: e.dma_start(
                            out=KB[b][:], in_=ck_d[sg].rearrange("(mc q) c -> q mc c", q=128)), w=[KBb[b]])
                        for half in range(2):
                            t, tb = pb()
                            tv = t[:, :].rearrange("p (k n) -> p k n", k=4)
                            for cq in range(4):
                                c = half * 4 + cq
                                for mc in range(2):
                                    S.op("pe", lambda e, cq=cq, c=c, mc=mc, b=b: e.transpose(
                                        out=tv[:, cq, mc * 128:(mc + 1) * 128], in_=KB[b][:, mc, c * 128:(c + 1) * 128],
                                        identity=IDB[:]), r=[KBb[b], IDb], w=[tb])
                            S.op("act" if half else "dve",
                                 (lambda e, b=b, half=half, tv=tv: e.copy(out=KTs[b][:, half * 4:half * 4 + 4, :], in_=tv)) if half else
                                 (lambda e, b=b, half=half, tv=tv: e.tensor_copy(out=KTs[b][:, half * 4:half * 4 + 4, :], in_=tv)),
                                 r=[tb], w=[KTsb[b]])
                        for h in range(4):
                            for ec in range(2):
                                S.op("pe", lambda e, h=h, ec=ec, s=s, b=b: e.matmul(
                                    shead[h][0], lhsT=QM[:, s, 2 * h + ec, :], rhs=KTs[b][:, 2 * h + ec, :],
                                    start=(s == 0 and ec == 0), stop=(s == 7 and ec == 1)), r=[QMb, KTsb[b]], w=[shead[h][1]])
                    b = softmax(shead, 32)
                    p_transpose(b, 32, 512)
                    tO, tOb = pz()
                    for s in range(8):
                        b = s % 2
                        sg = 8 * p + s
                        S.dma("pool", lambda e, b=b, sg=sg: e.dma_start(
                            out=VS[b][:], in_=cv_d[sg].rearrange("(mc q) c -> q mc c", q=128)), w=[VSb[b]])
                        for c in range(8):
                            h = c // 2
                            for mc in range(2):
                                S.op("pe", lambda e, c=c, h=h, mc=mc, s=s, b=b: e.matmul(
                                    tO[:, c * 32 + 4 * s:c * 32 + 4 * s + 4], lhsT=VS[b][:, mc, c * 128:(c + 1) * 128],
                                    rhs=PT[:, 2 * h + mc, 512 + 4 * s:516 + 4 * s], start=(mc == 0), stop=(mc == 1)),
                                    r=[VSb[b], PTb], w=[tOb])
                    S.op("act", lambda e: e.copy(out=OSs[:], in_=tO[:, 0:256].rearrange("p (c t) -> p c t", c=8)), r=[tOb], w=[OSsb])
                    for i in range(4):
                        view, sb, sl = w_get("w_in", COL_GC + i * 256)
                        for ci in range(2):
                            c = 2 * i + ci
                            b = c % 2
                            h = c // 2
                            t, tb = proj(view, sb, ci, 32, 32, HT, HTb, NT)
                            S.op("act", lambda e, b=b, t=t: e.activation(out=SGC[b][:], in_=t[:, 0:W], func=AF.Silu), r=[tb], w=[SGCb[b]])
                            o, ob = pz()
                            for mc in range(2):
                                S.op("pe", lambda e, mc=mc, c=c, h=h, o=o: e.matmul(
                                    o[:, 0:512], lhsT=VB[:, mc, c * 128:(c + 1) * 128], rhs=PT[:, 2 * h + mc, 0:512],
                                    start=(mc == 0), stop=(mc == 1)), r=[VBb, PTb], w=[ob])
                            S.op("dve", lambda e, c=c, b=b, o=o: e.tensor_tensor(
                                out=CC[:, 24 + c, 0:512], in0=o[:, 0:512], in1=SGC[b][:, 0:512], op=ALU.mult),
                                r=[ob, SGCb[b]], w=[CCb[24 + c]])
                            S.op("dve", lambda e, c=c, b=b: e.tensor_tensor(
                                out=CC[:, 24 + c, 512:W], in0=OSs[:, c, :], in1=SGC[b][:, 512:W], op=ALU.mult),
                                r=[OSsb, SGCb[b], CCb[24 + c]], w=[CCb[24 + c]])
                        w_release(sl)
                    S.barrier()

        for p in range(NPASS):
            S.new_epoch()
            UMODE[0] = "pz"
            run("p0", p)
            if p == 0:
                run("kv", p)
            UMODE[0] = "pb"
            run("b", p)
            UMODE[0] = "all"
            do_pull(1000)
            if p > 0 and "out" in phases:
                FPEND[0] = p - 1
            run("a", p)
            run("c", p)
            if "out" in phases and p < NPASS - 1:
                GEN[0] = out_main(p)
        if "out" in phases:
            with contextlib.ExitStack() as A:
                p = NPASS - 1
                HTF = HT.bitcast(F32)[:, :, :].rearrange("p a b -> p (a b)")
                YLt = [T("YL%d" % i, [128, D], F32, A) for i in range(3)]
                YL = [YLt[0], YLt[1], YLt[2], HTF[:, 0:D], HTF[:, D:2 * D]]
                YLb = [Buf("YL0"), Buf("YL1"), Buf("YL2"), HTb, HTb]
                GEN[0] = out_main(p, YL, YLb)
                do_pull(1000)
                final_sb(p, YL, YLb, A)
                S.barrier()
        S.barrier(pe_waits=True)
    return nc, rec


def _host_inputs(inp):
    f = lambda a: np.ascontiguousarray(np.asarray(a, dtype=np.float32))
    xp, xs, mem = f(inp["x_prompt"]), f(inp["x_sample"]), f(inp["mem_prompt"])
    ck = f(inp["cache_mem_k"])[0].reshape(128, 256, 1024)
    cv = f(inp["cache_mem_v"])[0].reshape(128, 256, 1024)
    sp = f(inp["state_pool"])[0]
    sc = f(inp["state_conv"])[0]
    shared = {
        "w_in": f(inp["w_in"])[0], "w_mem_k": f(inp["w_mem_k"])[0], "w_mem_v": f(inp["w_mem_v"])[0],
        "w_pool": f(inp["w_pool"])[0], "w_pw": f(inp["w_pw"])[0], "w_out": f(inp["w_out"])[0],
        "ident": np.eye(128, dtype=np.float32),
        "fg": np.ascontiguousarray(np.broadcast_to(f(inp["final_norm_g"])[None, :], (128, D))),
    }
    cbase = np.zeros((128, C_END), np.float32)
    cbase[:, C_G:C_G + 32] = f(inp["norm_g"])[0].reshape(32, 128).T
    cbase[:, C_MG:C_MG + 32] = f(inp["mem_norm_g"])[0].reshape(32, 128).T
    cbase[:, C_PSC:C_PSC + 12] = f(inp["pool_scale"])[0].reshape(12, 128).T
    cbase[:, C_BDW:C_BDW + 12] = f(inp["b_dw"])[0].reshape(12, 128).T
    cbase[:, C_LNG:C_LNG + 12] = f(inp["conv_ln_g"])[0].reshape(12, 128).T
    cbase[:, C_LNB:C_LNB + 12] = f(inp["conv_ln_b"])[0].reshape(12, 128).T
    cbase[:, C_WDW:C_WDW + 372] = f(inp["w_dw"])[0].T.reshape(12, 128, 31).transpose(1, 0, 2).reshape(128, 372)
    in_maps = []
    for c in range(NCORES):
        b, half = c // 2, c % 2
        x = np.zeros((NPASS, NT, D), np.float32)
        cs = cbase.copy()
        for p in range(NPASS):
            s0 = half * 1024 + p * 512
            if s0 > 0:
                x[p, 0:32] = xp[b, s0 - 32:s0]
            x[p, 32:544] = xp[b, s0:s0 + 512]
            x[p, 544:576] = xs[16 * c + 8 * p:16 * c + 8 * p + 8].reshape(32, D)
            for g, wdt in enumerate(POOLW):
                pos = s0 + np.arange(16)
                cs[:, C_INV + (p * 4 + g) * 16:C_INV + (p * 4 + g + 1) * 16] = (
                    1.0 / np.minimum(wdt, pos + 1).astype(np.float32))[None, :]
        m = dict(shared)
        m.update({"x": x, "mem": mem[b], "ck": ck[16 * c:16 * c + 16], "cv": cv[16 * c:16 * c + 16],
                  "sp_in": sp[16 * c:16 * c + 16], "sc_in": sc[16 * c:16 * c + 16], "consts": cs})
        in_maps.append(m)
    return in_maps


_NC_CACHE = {}


def kernel(**inputs):
    in_maps = _host_inputs(inputs)
    if "nc" not in _NC_CACHE:
        _, rec = build_program()
        _NC_CACHE["nc"] = build_program(loads_rec=rec)[0]
    nc = _NC_CACHE["nc"]
    res = run_bass_kernel_spmd(nc, in_maps, core_ids=list(range(NCORES)))
    R = res.results
    y_prompt = np.zeros((4, 2048, D), np.float32)
    y_sample = np.zeros((128, 4, D), np.float32)
    mk = np.zeros((1, 4, 256, 4, 256), np.float32)
    mv = np.zeros((1, 4, 256, 4, 256), np.float32)
    psp = np.zeros((1, 4, 15, 1536), np.float32)
    csp = np.zeros((1, 4, 30, 1536), np.float32)
    pss = np.zeros((1, 128, 15, 1536), np.float32)
    css = np.zeros((1, 128, 30, 1536), np.float32)
    for c in range(NCORES):
        b, half = c // 2, c % 2
        r = R[c]
        for p in range(NPASS):
            s0 = half * 1024 + p * 512
            y_prompt[b, s0:s0 + 512] = r["y"][p, 0:512]
            y_sample[16 * c + 8 * p:16 * c + 8 * p + 8] = r["y"][p, 512:544].reshape(8, 4, D)
        if half == 0:
            mk[0, b] = r["mk"].reshape(256, 4, 256)
            mv[0, b] = r["mv"].reshape(256, 4, 256)
        else:
            psp[0, b] = r["ps_p"][1]
            csp[0, b] = r["cs_p"][1]
        pss[0, 16 * c:16 * c + 16] = r["ps_s"]
        css[0, 16 * c:16 * c + 16] = r["cs_s"]
    return (y_prompt, y_sample, mk, mv, psp, csp, pss, css)
```
